# Optimizing a Trainium2 kernel written in Bass

```python
import jax, jax.numpy as jnp
from jax import lax
import numpy as np

D_MODEL = 1024
BATCH = 8
SEQ = 8192
DEPTH = 2

HEAD_DIM = 64
SWA_Q_HEADS = 8
SWA_KV_HEADS = 2
SWA_WINDOW = 128
FOX_HEADS = 8
BLOCK = 128
D_FF = 2816
CONV_WIDTH = 3
LN_EPS = 1e-5
NEG_INF = -1e30
FOX_GATE_BIAS_INIT = 3.0
DEEPNORM_ALPHA = (2 * DEPTH) ** 0.25
DEEPNORM_BETA = (8 * DEPTH) ** -0.25
SWA_Q = SWA_Q_HEADS * HEAD_DIM
SWA_KV = SWA_KV_HEADS * HEAD_DIM
FOX_W = FOX_HEADS * HEAD_DIM
SPLIT_SIZES = (SWA_Q, SWA_KV, SWA_KV, FOX_W, FOX_W, FOX_W, FOX_HEADS, D_MODEL, D_MODEL)
N_IN = SWA_Q + 2 * SWA_KV + 3 * FOX_W + FOX_HEADS + 2 * D_MODEL

kernel_name = "hybrid_swa_sink_fox_gated_convffn_deepnorm"


def layer_norm(x, g, b):
    xf = x.astype(jnp.float32)
    mu = jnp.mean(xf, axis=-1, keepdims=True)
    var = jnp.mean(jnp.square(xf - mu), axis=-1, keepdims=True)
    y = (xf - mu) * lax.rsqrt(var + LN_EPS) * g.astype(jnp.float32) + b.astype(jnp.float32)
    return y.astype(x.dtype)


def alibi_slopes(n_heads):
    return jnp.asarray(2.0 ** (-8.0 * np.arange(1, n_heads + 1) / n_heads), dtype=jnp.float32)


def swa_sink_attention(q, k, v, sinks):
    B, S, Hq, d = q.shape
    Hkv = k.shape[2]
    G = Hq // Hkv
    nb = S // BLOCK
    qb = q.reshape(B, nb, BLOCK, Hkv, G, d)
    kb = k.reshape(B, nb, BLOCK, Hkv, d)
    vb = v.reshape(B, nb, BLOCK, Hkv, d)

    def with_prev(t):
        prev = jnp.pad(t, ((0, 0), (1, 0), (0, 0), (0, 0), (0, 0)))[:, :-1]
        return jnp.concatenate([prev, t], axis=2)

    kw, vw = with_prev(kb), with_prev(vb)
    scores = jnp.einsum('bnqhgd,bnkhd->bnhgqk', qb, kw).astype(jnp.float32) * (d ** -0.5)
    q_pos = jnp.arange(BLOCK)[:, None] + BLOCK
    k_pos = jnp.arange(2 * BLOCK)[None, :]
    dist = q_pos - k_pos
    valid = (dist >= 0) & (dist < SWA_WINDOW)
    blk = jnp.arange(nb)[:, None, None]
    valid = valid[None] & ((k_pos[None] >= BLOCK) | (blk > 0))
    slopes = alibi_slopes(Hq).reshape(Hkv, G)
    alibi = -slopes[:, :, None, None] * dist.astype(jnp.float32)[None, None]
    scores = scores + alibi[None, None]
    scores = jnp.where(valid[None, :, None, None], scores, NEG_INF)
    sink = jnp.broadcast_to(sinks.astype(jnp.float32).reshape(Hkv, G)[None, None, :, :, None, None],
                            scores.shape[:-1] + (1,))
    probs = jax.nn.softmax(jnp.concatenate([scores, sink], axis=-1), axis=-1)[..., :-1]
    out = jnp.einsum('bnhgqk,bnkhd->bnqhgd', probs.astype(v.dtype), vw)
    return out.reshape(B, S, Hq * d)


def forgetting_attention(q, k, v, log_f):
    B, S, H, d = q.shape
    nb = S // BLOCK
    c = jnp.cumsum(log_f, axis=1)
    c_k = jnp.transpose(c, (0, 2, 1))
    qb = jnp.transpose(q.reshape(B, nb, BLOCK, H, d), (1, 0, 2, 3, 4))
    cq = jnp.transpose(c.reshape(B, nb, BLOCK, H), (1, 0, 3, 2))
    k_pos = jnp.arange(S)
    scale = d ** -0.5

    def one_block(args):
        q_blk, c_blk, i = args
        s = jnp.einsum('bqhd,bkhd->bhqk', q_blk, k).astype(jnp.float32) * scale
        s = s + c_blk[..., None] - c_k[:, :, None, :]
        q_pos = i * BLOCK + jnp.arange(BLOCK)
        mask = k_pos[None, :] <= q_pos[:, None]
        s = jnp.where(mask[None, None], s, NEG_INF)
        p = jax.nn.softmax(s, axis=-1)
        return jnp.einsum('bhqk,bkhd->bqhd', p.astype(v.dtype), v)

    out = lax.map(one_block, (qb, cq, jnp.arange(nb)))
    return jnp.transpose(out, (1, 0, 2, 3, 4)).reshape(B, S, H * d)


def token_mixer(h, w_in, b_in, sinks, w_proj_a, w_proj_b, w_out):
    B, S, _ = h.shape
    z = h @ w_in + b_in
    idx = [int(i) for i in np.cumsum(SPLIT_SIZES)[:-1]]
    q_a, k_a, v_a, q_b, k_b, v_b, f_logit, g_a, g_b = jnp.split(z, idx, axis=-1)
    y_a = swa_sink_attention(q_a.reshape(B, S, SWA_Q_HEADS, HEAD_DIM),
                             k_a.reshape(B, S, SWA_KV_HEADS, HEAD_DIM),
                             v_a.reshape(B, S, SWA_KV_HEADS, HEAD_DIM), sinks) @ w_proj_a
    log_f = jax.nn.log_sigmoid(f_logit.astype(jnp.float32))
    y_b = forgetting_attention(q_b.reshape(B, S, FOX_HEADS, HEAD_DIM),
                               k_b.reshape(B, S, FOX_HEADS, HEAD_DIM),
                               v_b.reshape(B, S, FOX_HEADS, HEAD_DIM), log_f) @ w_proj_b
    merged = jax.nn.sigmoid(g_a) * y_a + jax.nn.sigmoid(g_b) * y_b
    return merged @ w_out


def conv_gated_ffn(h, w_ffn_in, conv_w, conv_b, w_ffn_out):
    S = h.shape[1]
    gate, up = jnp.split(h @ w_ffn_in, 2, axis=-1)
    gp = jnp.pad(gate, ((0, 0), (CONV_WIDTH - 1, 0), (0, 0)))
    conv = conv_b
    for j in range(CONV_WIDTH):
        conv = conv + conv_w[j] * gp[:, j:j + S]
    return (jax.nn.silu(conv) * up) @ w_ffn_out


def setup_inputs(seed: int = 0) -> dict:
    key = jax.random.key(seed)
    ks = jax.random.split(key, 16)
    f32 = jnp.float32
    L = DEPTH
    beta = DEEPNORM_BETA
    x = jax.random.normal(ks[0], (BATCH, SEQ, D_MODEL), f32)
    ln_mix_g = 1.0 + 0.02 * jax.random.normal(ks[1], (L, D_MODEL), f32)
    ln_mix_b = 0.02 * jax.random.normal(ks[2], (L, D_MODEL), f32)
    col_scale = np.ones((N_IN,), np.float32)
    va0 = SWA_Q + SWA_KV
    col_scale[va0:va0 + SWA_KV] = beta
    vb0 = SWA_Q + 2 * SWA_KV + 2 * FOX_W
    col_scale[vb0:vb0 + FOX_W] = beta
    w_in = jax.random.normal(ks[3], (L, D_MODEL, N_IN), f32) * (D_MODEL ** -0.5) * jnp.asarray(col_scale)
    bias_off = np.zeros((N_IN,), np.float32)
    f0 = SWA_Q + 2 * SWA_KV + 3 * FOX_W
    bias_off[f0:f0 + FOX_HEADS] = FOX_GATE_BIAS_INIT
    b_in = 0.02 * jax.random.normal(ks[4], (L, N_IN), f32) + jnp.asarray(bias_off)
    attn_sinks = 0.5 * jax.random.normal(ks[5], (L, SWA_Q_HEADS), f32)
    w_proj_a = jax.random.normal(ks[6], (L, SWA_Q, D_MODEL), f32) * (SWA_Q ** -0.5) * beta
    w_proj_b = jax.random.normal(ks[7], (L, FOX_W, D_MODEL), f32) * (FOX_W ** -0.5) * beta
    w_out = jax.random.normal(ks[8], (L, D_MODEL, D_MODEL), f32) * (D_MODEL ** -0.5) * beta
    ln_ffn_g = 1.0 + 0.02 * jax.random.normal(ks[9], (L, D_MODEL), f32)
    ln_ffn_b = 0.02 * jax.random.normal(ks[10], (L, D_MODEL), f32)
    w_ffn_in = jax.random.normal(ks[11], (L, D_MODEL, 2 * D_FF), f32) * (D_MODEL ** -0.5) * beta
    conv_w = jax.random.normal(ks[12], (L, CONV_WIDTH, D_FF), f32) * (CONV_WIDTH ** -0.5)
    conv_b = 0.02 * jax.random.normal(ks[13], (L, D_FF), f32)
    w_ffn_out = jax.random.normal(ks[14], (L, D_FF, D_MODEL), f32) * (D_FF ** -0.5) * beta
    return {"x": x, "ln_mix_g": ln_mix_g, "ln_mix_b": ln_mix_b, "w_in": w_in, "b_in": b_in,
            "attn_sinks": attn_sinks, "w_proj_a": w_proj_a, "w_proj_b": w_proj_b, "w_out": w_out,
            "ln_ffn_g": ln_ffn_g, "ln_ffn_b": ln_ffn_b, "w_ffn_in": w_ffn_in, "conv_w": conv_w,
            "conv_b": conv_b, "w_ffn_out": w_ffn_out}


def reference(x, ln_mix_g, ln_mix_b, w_in, b_in, attn_sinks, w_proj_a, w_proj_b, w_out,
              ln_ffn_g, ln_ffn_b, w_ffn_in, conv_w, conv_b, w_ffn_out):
    h = x
    for l in range(DEPTH):
        mix = token_mixer(h, w_in[l], b_in[l], attn_sinks[l], w_proj_a[l], w_proj_b[l], w_out[l])
        h = layer_norm(DEEPNORM_ALPHA * h + mix, ln_mix_g[l], ln_mix_b[l])
        ffn = conv_gated_ffn(h, w_ffn_in[l], conv_w[l], conv_b[l], w_ffn_out[l])
        h = layer_norm(DEEPNORM_ALPHA * h + ffn, ln_ffn_g[l], ln_ffn_b[l])
    return h
```

```python
from contextlib import ExitStack
import numpy as np
import concourse.bass as bass
import concourse.mybir as mybir
from concourse.bass_utils import run_bass_kernel_spmd

F32 = mybir.dt.float32
BF16 = mybir.dt.bfloat16
AF = mybir.ActivationFunctionType
ALU = mybir.AluOpType

D = 1024
S = 8192
DEPTH = 2
TT = 512
NT = S // TT
NB = S // 128
KC = 8
DFF = 2816
FC = DFF // 128
NIN = 4360
ALPHA = float((2 * DEPTH) ** 0.25)
EPS = 1e-5
O_QA, O_KA, O_VA, O_QB, O_KB, O_VB, O_F, O_GA, O_GB = 0, 512, 640, 768, 1280, 1792, 2304, 2312, 3336


class SemObj:
    def __init__(self, nc, name):
        self.cm = nc.semaphore(name)
        self.sem = self.cm.__enter__()
        self.cnt = 0


class Eng:
    def __init__(self, nc, eng, name):
        self.e = eng
        self.s = SemObj(nc, "s_" + name)
        self.waited = {}

    def wait(self, *toks):
        for tok in toks:
            if tok is None:
                continue
            if isinstance(tok, list):
                self.wait(*tok)
                continue
            so, v = tok
            if self.waited.get(id(so), 0) >= v:
                continue
            self.waited[id(so)] = v
            self.e.wait_ge(so.sem, v)

    def sig(self, ins):
        self.s.cnt += 1
        ins.then_inc(self.s.sem, 1)
        return (self.s, self.s.cnt)

    def dma(self, out, in_, so, **kw):
        so.cnt += 16
        self.e.dma_start(out=out, in_=in_, **kw).then_inc(so.sem, 16)
        return (so, so.cnt)


class Ring:
    def __init__(self, bufs):
        self.bufs = bufs
        self.free = [None] * len(bufs)
        self.i = 0

    def get(self):
        idx = self.i % len(self.bufs)
        self.i += 1
        return idx, self.bufs[idx], self.free[idx]


def build_program(dbg=False, n_layers=DEPTH, stop_phase=99):
    nc = bass.Bass("TRN2", target_bir_lowering=False)

    def din(name, shape, dt=F32):
        return nc.dram_tensor(name, list(shape), dt, kind="ExternalInput").ap()

    def dscr(name, shape, dt, out=False):
        return nc.dram_tensor(name, list(shape), dt, kind="ExternalOutput" if (out and dbg) else "Internal").ap()

    xT = din("xT", [D, S])
    w_in = din("w_in", [DEPTH, D, NIN])
    b_in = din("b_in", [DEPTH, NIN])
    sinks = din("attn_sinks", [DEPTH, 8])
    w_pa = din("w_proj_a", [DEPTH, 512, D])
    w_pb = din("w_proj_b", [DEPTH, 512, D])
    w_o = din("w_out", [DEPTH, D, D])
    ln1g = din("ln_mix_g", [DEPTH, D])
    ln1b = din("ln_mix_b", [DEPTH, D])
    ln2g = din("ln_ffn_g", [DEPTH, D])
    ln2b = din("ln_ffn_b", [DEPTH, D])
    w_fi = din("w_ffn_in", [DEPTH, D, 2 * DFF])
    cw = din("conv_w", [DEPTH, 3, DFF])
    cb = din("conv_b", [DEPTH, DFF])
    w_fo = din("w_ffn_out", [DEPTH, DFF, D])
    b_q = din("b_q", [DEPTH, 128, 13])
    b_g = din("b_g", [DEPTH, 128, 16])
    b_f = din("b_f", [DEPTH, 8, 1])
    b_v = din("b_v", [DEPTH, 1, 640])
    ln1g_f = din("ln1g_f", [DEPTH, 128, KC])
    ln1b_f = din("ln1b_f", [DEPTH, 128, KC])
    ln2g_f = din("ln2g_f", [DEPTH, 128, KC])
    ln2b_f = din("ln2b_f", [DEPTH, 128, KC])
    cw_f = din("cw_f", [DEPTH, 128, 3, FC])
    cb_f = din("cb_f", [DEPTH, 128, FC])
    c_ident = din("c_ident", [128, 128])
    c_negm = din("c_negm", [128, 128])
    c_swam = din("c_swam", [128, 2, 2, 512])
    outT = nc.dram_tensor("outT", [D, S], F32, kind="ExternalOutput").ap()

    QAT = dscr("QAT", [512, S], BF16)
    KAT = dscr("KAT", [128, S], BF16)
    VA = dscr("VA", [S, 128], BF16)
    QBT = dscr("QBT", [8, 70, S], BF16)
    KBT = dscr("KBT", [8, 70, S], BF16)
    VB = dscr("VB", [S, 512], BF16)
    OAT = dscr("OAT", [512, S], BF16, out=True)
    OBT = dscr("OBT", [512, S], BF16, out=True)
    H1T = dscr("H1T", [D, S], F32, out=True)
    H2T = dscr("H2T", [D, S], F32, out=True)
    ACTT = dscr("ACTT", [DFF, S], BF16, out=True)
    CDBG = dscr("CDBG", [8, S], F32, out=True)

    PE = Eng(nc, nc.tensor, "pe")
    ACT = Eng(nc, nc.scalar, "act")
    DVE = Eng(nc, nc.vector, "dve")
    POOL = Eng(nc, nc.gpsimd, "pool")
    SP = Eng(nc, nc.sync, "sp")
    ENGS = [PE, ACT, DVE, POOL, SP]
    sems = {}
    ucnt = [0]

    def uniq(name):
        ucnt[0] += 1
        return f"{name}_{ucnt[0]}"

    def so(name):
        if name not in sems:
            sems[name] = SemObj(nc, name)
        return sems[name]

    top = ExitStack()
    identf = top.enter_context(nc.sbuf_tensor("identf", [128, 128], F32))
    identb = top.enter_context(nc.sbuf_tensor("identb", [128, 128], BF16))
    negmb = top.enter_context(nc.sbuf_tensor("negmb", [128, 128], BF16))
    onesf = top.enter_context(nc.sbuf_tensor("onesf", [128, 128], F32))
    tiny = top.enter_context(nc.sbuf_tensor("tiny", [128, 8], F32))

    t0 = SP.dma(identf[:], c_ident, so("c0"))
    t1 = POOL.dma(identb[:], c_ident, so("c1"))
    t2 = POOL.dma(negmb[:], c_negm, so("c2"))
    DVE.wait(t0, t1, t2)
    nc.vector.memset(onesf[:], 1.0)
    t_const = DVE.sig(nc.vector.memset(tiny[:], 0.0))
    for E in ENGS:
        E.wait(t_const)

    def barrier(extra=()):
        toks = list(extra)
        toks.append(ACT.sig(nc.scalar.activation(out=tiny[0:1, 0:1], in_=tiny[0:1, 4:5], func=AF.Identity)))
        toks.append(DVE.sig(nc.vector.memset(tiny[0:1, 1:2], 0.0)))
        toks.append(POOL.sig(nc.gpsimd.memset(tiny[0:1, 2:3], 0.0)))
        for E in ENGS:
            E.wait(*toks)

    def load_w_cast(dst3, src2, nchunk, ncols, semname, col_split=2048, order=None):
        toks = []
        srcv = src2.rearrange("(c p) n -> p c n", p=128)
        starts = list(range(0, ncols, col_split))
        if order is None:
            for c0 in starts:
                c1 = min(ncols, c0 + col_split)
                toks.append(POOL.dma(dst3[:, :, c0:c1], srcv[:, :, c0:c1], so(semname)))
            return toks[-1]
        out = {}
        for ci in order:
            c0 = starts[ci]
            c1 = min(ncols, c0 + col_split)
            out[ci] = POOL.dma(dst3[:, :, c0:c1], srcv[:, :, c0:c1], so(f"{semname}_{ci}"))
        return out

    class LNState:
        def __init__(self, lnbufs, r, outf, g_t, b_t, hdst_v, sempfx):
            self.mean_s, self.msq, self.var, self.rstd, self.nmr, self.tmp = lnbufs
            self.r, self.outf, self.g_t, self.b_t, self.hdst_v, self.sempfx = r, outf, g_t, b_t, hdst_v, sempfx
            self.pending = None
            self.tmp_free = [None, None]
            self.st_tok = [None] * KC
            self.pre_tok = [None] * KC
            self.last_dve = None
            self.last_pool = None
            self.stats_free = None
            self.t_nmr = None

        def finalize(self, tsl, ps_sum, ps_sq, t_stats):
            assert self.pending is None
            ACT.wait(t_stats)
            t_mean = ACT.sig(nc.scalar.activation(out=self.mean_s[:], in_=ps_sum[:], func=AF.Identity, scale=1.0 / D))
            DVE.wait(t_mean, t_stats)
            t_msq = DVE.sig(nc.vector.tensor_tensor(out=self.msq[:], in0=self.mean_s[:], in1=self.mean_s[:], op=ALU.mult))
            DVE.wait(t_msq)
            t_var = DVE.sig(nc.vector.scalar_tensor_tensor(out=self.var[:], in0=ps_sq[:], scalar=1.0 / D, in1=self.msq[:], op0=ALU.mult, op1=ALU.subtract))
            ACT.wait(t_var)
            t_ln = ACT.sig(nc.scalar.activation(out=self.var[:], in_=self.var[:], func=AF.Ln, bias=eps_t[:, 0:1], scale=1.0))
            ACT.wait(t_ln, self.last_dve)
            t_rstd = ACT.sig(nc.scalar.activation(out=self.rstd[:], in_=self.var[:], func=AF.Exp, scale=-0.5))
            self.stats_free = t_rstd
            DVE.wait(t_rstd, self.last_pool)
            self.t_nmr = DVE.sig(nc.vector.scalar_tensor_tensor(out=self.nmr[:], in0=self.mean_s[:], scalar=-1.0, in1=self.rstd[:], op0=ALU.mult, op1=ALU.mult))
            self.pending = tsl

        def apply_pre(self, m):
            if self.pending is None:
                return None
            tb = self.tmp[m % 2]
            DVE.wait(self.tmp_free[m % 2], self.t_nmr)
            t_a = DVE.sig(nc.vector.tensor_tensor(out=tb[:], in0=self.r[:, m, :], in1=self.rstd[:], op=ALU.mult))
            POOL.wait(t_a, self.t_nmr)
            t_t = POOL.sig(nc.gpsimd.tensor_tensor(out=tb[:], in0=tb[:], in1=self.nmr[:], op=ALU.add))
            self.pre_tok[m] = t_t
            self.last_dve, self.last_pool = t_a, t_t
            return t_a

        def apply_post(self, m):
            if self.pending is None or self.pre_tok[m] is None:
                return
            tsl = self.pending
            tb = self.tmp[m % 2]
            ACT.wait(self.pre_tok[m], self.st_tok[m])
            t_o = ACT.sig(nc.scalar.activation(out=self.outf[:, m, :], in_=tb[:], func=AF.Identity,
                                               scale=self.g_t[:, m:m + 1], bias=self.b_t[:, m:m + 1]))
            self.tmp_free[m % 2] = t_o
            self.pre_tok[m] = None
            SP.wait(t_o)
            self.st_tok[m] = SP.dma(self.hdst_v[:, m, tsl], self.outf[:, m, :], so(f"{self.sempfx}{m}"))
            if m == KC - 1:
                self.pending = None

        def apply(self, m):
            t_a = self.apply_pre(m)
            self.apply_post(m)
            return t_a

        def flush(self):
            for m in range(KC):
                self.apply(m)
            return [t for t in self.st_tok if t is not None]

    def make_stats(r, sq, acc_r, acc_q, ps_sum, ps_sq, LN):
        st = {"ar": None, "aq": None, "pe": None, "deferred": None}

        def stats_chunk(m, t_r, tsl):
            qi, sqb, qfree = sq.get()
            ACT.wait(t_r, qfree)
            t_q = ACT.sig(nc.scalar.activation(out=sqb[:], in_=r[:, m, :], func=AF.Square))
            if m == 0:
                DVE.wait(t_r, st["pe"])
                st["ar"] = DVE.sig(nc.vector.tensor_copy(out=acc_r[:], in_=r[:, 0, :]))
                POOL.wait(t_q, st["pe"])
                st["aq"] = POOL.sig(nc.gpsimd.tensor_copy(out=acc_q[:], in_=sqb[:]))
            else:
                DVE.wait(t_r, st["ar"])
                st["ar"] = DVE.sig(nc.vector.tensor_tensor(out=acc_r[:], in0=acc_r[:], in1=r[:, m, :], op=ALU.add))
                if m == KC - 1:
                    DVE.wait(t_q, st["aq"])
                    st["aq"] = DVE.sig(nc.vector.tensor_tensor(out=acc_q[:], in0=acc_q[:], in1=sqb[:], op=ALU.add))
                else:
                    POOL.wait(t_q, st["aq"])
                    st["aq"] = POOL.sig(nc.gpsimd.tensor_tensor(out=acc_q[:], in0=acc_q[:], in1=sqb[:], op=ALU.add))
            sq.free[qi] = st["aq"]
            if m == KC - 1:
                t_ar, t_aq = st["ar"], st["aq"]

                def deferred():
                    PE.wait(t_ar, t_aq, LN.stats_free)
                    nc.tensor.matmul(ps_sum[:], lhsT=onesf[:], rhs=acc_r[:], start=True, stop=True)
                    st["pe"] = PE.sig(nc.tensor.matmul(ps_sq[:], lhsT=onesf[:], rhs=acc_q[:], start=True, stop=True))
                    LN.finalize(tsl, ps_sum, ps_sq, st["pe"])

                st["deferred"] = deferred

        def run_deferred():
            if st["deferred"] is not None:
                st["deferred"]()
                st["deferred"] = None

        return stats_chunk, run_deferred

    eps_t = top.enter_context(nc.sbuf_tensor("eps_t", [128, 1], F32))
    t_eps = DVE.sig(nc.vector.memset(eps_t[:], EPS))
    ACT.wait(t_eps)

    def phase1(l, hsrc):
        with ExitStack() as es:
            A = lambda name, shape, dt: es.enter_context(nc.sbuf_tensor(uniq(name), shape, dt))
            w1 = A("w1", [128, KC, 2312], BF16)
            bq = A("bq", [128, 13], F32)
            bfn = A("bfn", [8, 1], F32)
            bv = A("bv", [128, 640], F32)
            hf = [A(f"hf{i}", [128, KC, TT], F32) for i in range(2)]
            hb = [A(f"hb{i}", [128, KC, TT], BF16) for i in range(2)]
            st = Ring([A(f"st{i}", [128, TT], BF16) for i in range(4)])
            sv = Ring([A(f"sv{i}", [128, 4, 640], BF16) for i in range(2)])
            cc = A("cc", [8, S], F32)
            et = A("et", [8, TT], F32)
            lt = A("lt", [8, TT], F32)
            shb = Ring([(A(f"kp{i}", [8, 3, TT], BF16), A(f"qp{i}", [8, 3, TT], BF16), A(f"r1_{i}", [8, TT], F32), A(f"r2_{i}", [8, TT], F32)) for i in range(2)])
            ones8 = A("ones8", [8, 3, TT], BF16)
            eights = A("eights", [8, 3, TT], BF16)
            banks = Ring([es.enter_context(nc.psum_tensor(uniq(f"p1b{i}"), [128, 512], F32)) for i in range(6)])

            t_w = load_w_cast(w1, w_in[l, :, 0:2312], KC, 2312, "w1s", col_split=1156)
            so_b = so("p1bias")
            SP.dma(bq[:], b_q[l], so_b)
            SP.dma(bfn[:], b_f[l], so_b)
            t_b = SP.dma(bv[:], b_v[l].partition_broadcast(128), so_b)
            DVE.wait(t_b)
            nc.vector.tensor_scalar(out=bfn[:], in0=bfn[:], scalar1=-1.0, scalar2=None, op0=ALU.mult)
            nc.vector.memset(eights[:], 8.0)
            t_init = DVE.sig(nc.vector.memset(ones8[:], 1.0))
            ACT.wait(t_b, t_init)
            SP.wait(t_init)
            PE.wait(t_w)

            hsrc_v = hsrc.rearrange("(c p) t -> p c t", p=128)
            ld_sem = [so("p1ld0"), so("p1ld1")]
            hf_free = [None, None]
            hb_free = [None, None]
            ld_tok = [None, None]
            ld_tok[0] = SP.dma(hf[0][:], hsrc_v[:, :, 0:TT], ld_sem[0])
            store_toks = {}
            out_specs = []
            for c in range(4):
                out_specs.append((O_QA + c * 128, ("QA", c)))
            out_specs.append((O_KA, ("KA", 0)))
            for c in range(4):
                out_specs.append((O_QB + c * 128, ("QB", c)))
            for c in range(4):
                out_specs.append((O_KB + c * 128, ("KB", c)))
            prev_c_last = None
            cast_tok = [None, None]

            def emit_cast(T_):
                p_ = T_ % 2
                ACT.wait(ld_tok[p_], hb_free[p_])
                nc.scalar.activation(out=hb[p_][:, 0:4, :], in_=hf[p_][:, 0:4, :], func=AF.Identity)
                tk_ = ACT.sig(nc.scalar.activation(out=hb[p_][:, 4:8, :], in_=hf[p_][:, 4:8, :], func=AF.Identity))
                hf_free[p_] = tk_
                return tk_

            for T in range(NT):
                p = T % 2
                tsl = slice(T * TT, (T + 1) * TT)
                if T + 1 < NT:
                    SP.wait(hf_free[1 - p])
                    ld_tok[1 - p] = SP.dma(hf[1 - p][:], hsrc_v[:, :, (T + 1) * TT:(T + 2) * TT], ld_sem[1 - p])
                if T == 0:
                    cast_tok[0] = emit_cast(0)
                PE.wait(cast_tok[p])
                bi, bank, bfree = banks.get()
                PE.wait(bfree)
                for k in range(KC):
                    ins = nc.tensor.matmul(bank[0:8, :], lhsT=w1[:, k, O_F:O_F + 8], rhs=hb[p][:, k, :], start=(k == 0), stop=(k == KC - 1))
                t_mm = PE.sig(ins)
                ACT.wait(t_mm, prev_c_last)
                t_e = ACT.sig(nc.scalar.activation(out=et[:], in_=bank[0:8, :], func=AF.Exp, scale=-1.0, bias=bfn[:, 0:1]))
                ACT.wait(t_e)
                t_lf = ACT.sig(nc.scalar.activation(out=lt[:], in_=et[:], func=AF.Ln, bias=1.0, scale=1.0))
                banks.free[bi] = t_lf
                DVE.wait(t_lf)
                init = 0.0 if T == 0 else cc[:, T * TT - 1:T * TT]
                t_sc = DVE.sig(nc.vector.tensor_tensor_scan(out=cc[:, tsl], data0=lt[:], data1=lt[:], initial=init, op0=ALU.add, op1=ALU.bypass))
                hi, (kp, qp, r1, r2), shfree = shb.get()
                DVE.wait(shfree, t_sc)
                t_a = DVE.sig(nc.vector.tensor_copy(out=kp[:, 0, :], in_=cc[:, tsl]))
                DVE.wait(t_a)
                t_b_ = DVE.sig(nc.vector.tensor_tensor(out=r1[:], in0=cc[:, tsl], in1=kp[:, 0, :], op=ALU.subtract))
                DVE.wait(t_b_)
                t_c_ = DVE.sig(nc.vector.tensor_copy(out=kp[:, 1, :], in_=r1[:]))
                DVE.wait(t_c_)
                t_d_ = DVE.sig(nc.vector.tensor_tensor(out=r2[:], in0=r1[:], in1=kp[:, 1, :], op=ALU.subtract))
                DVE.wait(t_d_)
                t_e_ = DVE.sig(nc.vector.tensor_copy(out=kp[:, 2, :], in_=r2[:]))
                DVE.wait(t_e_)
                t_sh = DVE.sig(nc.vector.tensor_scalar(out=qp[:], in0=kp[:], scalar1=-8.0, scalar2=None, op0=ALU.mult))
                prev_c_last = t_sh
                SP.wait(t_sh)
                hsem = so(f"p1sh{hi}")
                SP.dma(KBT[:, 67:70, tsl], kp[:], hsem)
                SP.dma(QBT[:, 64:67, tsl], qp[:], hsem)
                SP.dma(KBT[:, 64:67, tsl], ones8[:], hsem)
                tk = SP.dma(QBT[:, 67:70, tsl], eights[:], hsem)
                shb.free[hi] = tk
                store_toks[("sh", hi)] = tk
                for oi, (wo, (kind, c)) in enumerate(out_specs):
                    if oi == 6 and T + 1 < NT:
                        cast_tok[1 - p] = emit_cast(T + 1)
                    bi, bank, bfree = banks.get()
                    PE.wait(bfree)
                    for k in range(KC):
                        ins = nc.tensor.matmul(bank[:], lhsT=w1[:, k, wo:wo + 128], rhs=hb[p][:, k, :], start=(k == 0), stop=(k == KC - 1))
                    t_mm = PE.sig(ins)
                    si, sbuf, sfree = st.get()
                    ACT.wait(t_mm, sfree)
                    t_ev = ACT.sig(nc.scalar.activation(out=sbuf[:], in_=bank[:], func=AF.Identity, bias=bq[:, oi:oi + 1], scale=1.0))
                    banks.free[bi] = t_ev
                    SP.wait(t_ev)
                    ssem = so(f"p1st{si}")
                    if kind == "QA":
                        tk = SP.dma(QAT[c * 128:(c + 1) * 128, tsl], sbuf[:], ssem)
                    elif kind == "KA":
                        tk = SP.dma(KAT[:, tsl], sbuf[:], ssem)
                    else:
                        dst = QBT if kind == "QB" else KBT
                        SP.dma(dst[2 * c, 0:64, tsl], sbuf[0:64, :], ssem)
                        tk = SP.dma(dst[2 * c + 1, 0:64, tsl], sbuf[64:128, :], ssem)
                    st.free[si] = tk
                    store_toks[("st", si)] = tk
                vi, svbuf, svfree = sv.get()
                DVE.wait(svfree)
                for b in range(4):
                    bi, bank, bfree = banks.get()
                    PE.wait(bfree)
                    for k in range(KC):
                        ins = nc.tensor.matmul(bank[:], lhsT=hb[p][:, k, b * 128:(b + 1) * 128], rhs=w1[:, k, O_VB:O_VB + 512], start=(k == 0), stop=(k == KC - 1))
                    t_mb = PE.sig(ins)
                    bi2, bank2, bfree2 = banks.get()
                    PE.wait(bfree2)
                    for k in range(KC):
                        ins = nc.tensor.matmul(bank2[:, 0:128], lhsT=hb[p][:, k, b * 128:(b + 1) * 128], rhs=w1[:, k, O_VA:O_VA + 128], start=(k == 0), stop=(k == KC - 1))
                    t_ma = PE.sig(ins)
                    DVE.wait(t_mb, t_ma)
                    t_e1 = DVE.sig(nc.vector.tensor_tensor(out=svbuf[:, b, 128:640], in0=bank[:], in1=bv[:, 128:640], op=ALU.add))
                    t_e2 = DVE.sig(nc.vector.tensor_tensor(out=svbuf[:, b, 0:128], in0=bank2[:, 0:128], in1=bv[:, 0:128], op=ALU.add))
                    banks.free[bi] = t_e1
                    banks.free[bi2] = t_e2
                hb_free[p] = t_ma
                SP.wait(t_e2)
                vsem = so(f"p1sv{vi}")
                SP.dma(VA[tsl, :].rearrange("(b p) d -> p b d", p=128), svbuf[:, :, 0:128], vsem)
                tk = SP.dma(VB[tsl, :].rearrange("(b p) d -> p b d", p=128), svbuf[:, :, 128:640], vsem)
                sv.free[vi] = tk
                store_toks[("sv", vi)] = tk
            if dbg:
                SP.wait(prev_c_last)
                store_toks["cdbg"] = SP.dma(CDBG, cc[:], so("cdbg"))
            barrier([v for v in store_toks.values()])

    def phase_swa(l):
        with ExitStack() as es:
            A = lambda name, shape, dt: es.enter_context(nc.sbuf_tensor(uniq(name), shape, dt))
            qa = A("qa", [64, 4, S], BF16)
            ka = A("ka", [64, S], BF16)
            va = A("va", [128, NB, 128], BF16)
            mk = A("mk", [128, 2, 2, 512], BF16)
            esk = A("esk", [128, 8], F32)
            zf = A("zf", [128, 128], F32)
            esk512 = A("esk512", [128, 2, 512], F32)
            pb = Ring([A(f"pb{i}", [128, 512], BF16) for i in range(4)])
            dnr = Ring([A(f"dn{i}", [128, 512], F32) for i in range(2)])
            lnr = Ring([A(f"lnr{i}", [128, 512], F32) for i in range(2)])
            rcr = Ring([A(f"rcr{i}", [128, 512], F32) for i in range(2)])
            ost = Ring([A(f"ost{i}", [64, 4, TT], BF16) for i in range(2)])
            sb = Ring([es.enter_context(nc.psum_tensor(uniq(f"swS{i}"), [128, 512], F32)) for i in range(4)])
            ob = Ring([es.enter_context(nc.psum_tensor(uniq(f"swO{i}"), [128, 512], F32)) for i in range(3)])
            so_c = so("swc")
            t_mk = POOL.dma(mk[:], c_swam, so("swmk"))
            t_c = SP.dma(esk[:], sinks[l].rearrange("(o n) -> o n", o=1).partition_broadcast(128), so_c)
            ACT.wait(t_c)
            t_es = ACT.sig(nc.scalar.activation(out=esk[:], in_=esk[:], func=AF.Exp))
            t_ones = POOL.sig(nc.gpsimd.memset(va[:, :, 64:128], 1.0))
            t_z = DVE.sig(nc.vector.memset(zf[:], 0.0))
            DVE.wait(t_es, t_z)
            for g in range(2):
                for hh in range(4):
                    t_e5 = DVE.sig(nc.vector.tensor_scalar(out=esk512[64:128, g, hh * 128:(hh + 1) * 128], in0=zf[64:128, :],
                                                           scalar1=esk[64:128, g * 4 + hh:g * 4 + hh + 1], scalar2=None, op0=ALU.add))
            DVE.wait(t_e5)
            PE.wait(t_mk)
            store_toks = {}
            last_use = None
            chunk_free = [None] * 4
            pending_norm = [None]
            NCH = 4
            CW = S // NCH
            ch_tok = [[None] * NCH, [None] * NCH]

            def emit_load(g_, c):
                SP.wait(chunk_free[c])
                so_l = so(f"swld{c}")
                csl = slice(c * CW, (c + 1) * CW)
                SP.dma(qa[:, 0:2, csl], QAT[g_ * 256:g_ * 256 + 128, csl].rearrange("(h d) t -> d h t", d=64), so_l)
                SP.dma(qa[:, 2:4, csl], QAT[g_ * 256 + 128:g_ * 256 + 256, csl].rearrange("(h d) t -> d h t", d=64), so_l)
                SP.dma(ka[:, csl], KAT[g_ * 64:(g_ + 1) * 64, csl], so_l)
                return SP.dma(va[:, c * (NB // NCH):(c + 1) * (NB // NCH), 0:64],
                              VA[csl, g_ * 64:(g_ + 1) * 64].rearrange("(n p) d -> p n d", p=128), so_l)

            for g in range(2):
                if g == 0:
                    SP.wait(t_ones)
                    for c in range(NCH):
                        ch_tok[0][c] = emit_load(0, c)
                oi = osb = None
                for n in range(NB):
                    if n % (NB // NCH) == 0:
                        PE.wait(ch_tok[g][n // (NB // NCH)])
                    if n % 4 == 0:
                        oi, osb, ofree = ost.get()
                        DVE.wait(ofree)
                    qsl = slice(n * 128, (n + 1) * 128)
                    parts = ([("prev", n - 1)] if n > 0 else []) + [("cur", n)]
                    pbs = []
                    for (which, kb) in parts:
                        midx = 0 if which == "prev" else 1
                        si, sbank, sfree = sb.get()
                        PE.wait(sfree)
                        nc.tensor.matmul(sbank[:].rearrange("p (h q) -> p h q", h=4), lhsT=ka[:, kb * 128:(kb + 1) * 128], rhs=qa[:, :, qsl], start=True, stop=False)
                        t_s = PE.sig(nc.tensor.matmul(sbank[:], lhsT=identb[:], rhs=mk[:, g, midx, :], start=False, stop=True))
                        bi_, pbb, pbfree = pb.get()
                        ACT.wait(t_s, pbfree)
                        t_p = ACT.sig(nc.scalar.activation(out=pbb[:], in_=sbank[:], func=AF.Exp, scale=0.125))
                        sb.free[si] = t_p
                        pbs.append((bi_, pbb, t_p, kb))
                    obi, obank, obfree = ob.get()
                    PE.wait(obfree)
                    for ii, (bi_, pbb, t_p, kb) in enumerate(pbs):
                        PE.wait(t_p)
                        t_o = PE.sig(nc.tensor.matmul(obank[:], lhsT=va[:, kb, :], rhs=pbb[:], start=(ii == 0), stop=(ii == len(pbs) - 1)))
                        pb.free[bi_] = t_o
                    di, dnb, dfree = dnr.get()
                    DVE.wait(t_o, dfree)
                    t_d = DVE.sig(nc.vector.tensor_tensor(out=dnb[64:128, :], in0=obank[64:128, :], in1=esk512[64:128, g, :], op=ALU.add))

                    def norm(n=n, g=g, obi=obi, obank=obank, di=di, dnb=dnb, t_d=t_d, oi=oi, osb=osb):
                        li, lnb_, lfree = lnr.get()
                        ACT.wait(t_d, lfree)
                        t_l = ACT.sig(nc.scalar.activation(out=lnb_[64:128, :], in_=dnb[64:128, :], func=AF.Ln))
                        dnr.free[di] = t_l
                        ri, rcb, rfree = rcr.get()
                        ACT.wait(t_l, rfree)
                        t_r = ACT.sig(nc.scalar.activation(out=rcb[64:128, :], in_=lnb_[64:128, :], func=AF.Exp, scale=-1.0))
                        lnr.free[li] = t_r
                        DVE.wait(t_r)
                        t_n = DVE.sig(nc.vector.tensor_tensor(out=osb[:, :, (n % 4) * 128:(n % 4 + 1) * 128],
                                                              in0=obank[0:64, :].rearrange("p (h q) -> p h q", h=4),
                                                              in1=rcb[64:128, :].rearrange("p (h q) -> p h q", h=4), op=ALU.mult))
                        ob.free[obi] = t_n
                        rcr.free[ri] = t_n
                        if n % 4 == 3:
                            SP.wait(t_n)
                            T = n // 4
                            osem = so(f"swst{oi}")
                            SP.dma(OAT[g * 256:g * 256 + 128, T * TT:(T + 1) * TT].rearrange("(h d) t -> d h t", d=64), osb[:, 0:2, :], osem)
                            tk = SP.dma(OAT[g * 256 + 128:g * 256 + 256, T * TT:(T + 1) * TT].rearrange("(h d) t -> d h t", d=64), osb[:, 2:4, :], osem)
                            ost.free[oi] = tk
                            store_toks[oi] = tk

                    if pending_norm[0] is not None:
                        pending_norm[0]()
                    pending_norm[0] = norm
                    last_use = t_o
                    if n % (NB // NCH) == 0 and n > 0:
                        chunk_free[n // (NB // NCH) - 1] = t_o
                        if g == 0:
                            ch_tok[1][n // (NB // NCH) - 1] = emit_load(1, n // (NB // NCH) - 1)
                    if n == NB - 1:
                        chunk_free[NCH - 1] = t_o
                        if g == 0:
                            ch_tok[1][NCH - 1] = emit_load(1, NCH - 1)
                if pending_norm[0] is not None:
                    pending_norm[0]()
                    pending_norm[0] = None
            barrier(list(store_toks.values()))

    def phase_fox(l, after_loads=None):
        with ExitStack() as es:
            A = lambda name, shape, dt: es.enter_context(nc.sbuf_tensor(uniq(name), shape, dt))
            qb = [A(f"fq{i}", [70, S], BF16) for i in range(2)]
            kb = [A(f"fk{i}", [70, S], BF16) for i in range(2)]
            vb = [A(f"fv{i}", [128, NB, 128], BF16) for i in range(2)]
            pr = Ring([A(f"fp{i}", [128, 2 * TT], BF16) for i in range(3)])
            rc = Ring([A(f"frc{i}", [128, 512], F32) for i in range(2)])
            ost = Ring([A(f"fo{i}", [64, 512], BF16) for i in range(2)])
            sall = es.enter_context(nc.psum_tensor(uniq("fS"), [128, 6 * TT], F32))
            sb = Ring([sall[:, i * 2 * TT:(i + 1) * 2 * TT] for i in range(3)])
            ob = Ring([es.enter_context(nc.psum_tensor(uniq(f"fO{i}"), [128, 512], F32)) for i in range(2)])
            nc.gpsimd.memset(vb[0][:, :, 64:128], 1.0)
            t_ones = POOL.sig(nc.gpsimd.memset(vb[1][:, :, 64:128], 1.0))
            SP.wait(t_ones)
            ld_tok = [None, None]
            buf_free = [None, None]
            store_toks = {}

            def load_head(h):
                p = h % 2
                SP.wait(buf_free[p])
                sl = so(f"fld{p}")
                SP.dma(qb[p][:], QBT[h], sl)
                SP.dma(kb[p][:], KBT[h], sl)
                ld_tok[p] = SP.dma(vb[p][:, :, 0:64], VB[:, h * 64:(h + 1) * 64].rearrange("(n p) d -> p n d", p=128), sl)

            load_head(0)
            if after_loads is not None:
                after_loads()
            for h in range(8):
                p = h % 2
                if h + 1 < 8:
                    load_head(h + 1)
                PE.wait(ld_tok[p])
                Q, K, V = qb[p], kb[p], vb[p]
                units = []
                for T in range(NT):
                    nj = 4 * T + 4
                    for j in range(0, 4 * T, 2):
                        units.append((T, j, nj, True))
                    for j in range(4 * T, nj):
                        units.append((T, j, nj, False))
                state = {}

                def emit_S(u):
                    T, j, nj, pair = u
                    si, sbank, sfree = sb.get()
                    PE.wait(sfree)
                    q0 = T * TT
                    if pair:
                        nc.tensor.matmul(sbank[:, 0:TT], lhsT=K[:, j * 128:(j + 1) * 128], rhs=Q[:, q0:q0 + TT], start=True, stop=True)
                        t_s = PE.sig(nc.tensor.matmul(sbank[:, TT:2 * TT], lhsT=K[:, (j + 1) * 128:(j + 2) * 128], rhs=Q[:, q0:q0 + TT], start=True, stop=True))
                        cols = (0, 2 * TT)
                    else:
                        c0 = (j - 4 * T) * 128
                        nc.tensor.matmul(sbank[:, c0:c0 + 128], lhsT=K[:, j * 128:(j + 1) * 128], rhs=Q[:, q0 + c0:q0 + c0 + 128], start=True, stop=False)
                        t_s = PE.sig(nc.tensor.matmul(sbank[:, c0:c0 + 128], lhsT=identb[:], rhs=negmb[:], start=False, stop=True))
                        if c0 + 128 < TT:
                            t_s = PE.sig(nc.tensor.matmul(sbank[:, c0 + 128:TT], lhsT=K[:, j * 128:(j + 1) * 128], rhs=Q[:, q0 + c0 + 128:q0 + TT], start=True, stop=True))
                        cols = (c0, TT)
                    pi, pbuf, pfree = pr.get()
                    ACT.wait(t_s, pfree)
                    a_, b_ = cols
                    t_p = ACT.sig(nc.scalar.activation(out=pbuf[:, a_:b_], in_=sbank[:, a_:b_], func=AF.Exp, scale=0.125))
                    sb.free[si] = t_p
                    return (pi, pbuf, t_p, cols)

                def emit_PV(u, sres):
                    T, j, nj, pair = u
                    pi, pbuf, t_p, (a_, b_) = sres
                    if j == 0:
                        obi, obank, obfree = ob.get()
                        PE.wait(obfree)
                        state[("o", T)] = (obi, obank)
                    obi, obank = state[("o", T)]
                    PE.wait(t_p)
                    if pair:
                        nc.tensor.matmul(obank[:], lhsT=V[:, j, :], rhs=pbuf[:, 0:TT], start=(j == 0), stop=False)
                        t_o = PE.sig(nc.tensor.matmul(obank[:], lhsT=V[:, j + 1, :], rhs=pbuf[:, TT:2 * TT], start=False, stop=False))
                    else:
                        t_o = PE.sig(nc.tensor.matmul(obank[:, a_:b_], lhsT=V[:, j, :], rhs=pbuf[:, a_:b_], start=(j == 0), stop=(j == nj - 1)))
                    pr.free[pi] = t_o
                    if j == nj - 1:
                        ri, rcb, rfree = rc.get()
                        DVE.wait(t_o, rfree)
                        t_rc = DVE.sig(nc.vector.reciprocal(out=rcb[64:128, :], in_=obank[64:128, :]))
                        oi, osb, ofree = ost.get()
                        DVE.wait(ofree, t_rc)
                        t_n = DVE.sig(nc.vector.tensor_tensor(out=osb[:], in0=obank[0:64, :], in1=rcb[64:128, :], op=ALU.mult))
                        ob.free[obi] = t_n
                        rc.free[ri] = t_n
                        SP.wait(t_n)
                        tk = SP.dma(OBT[h * 64:(h + 1) * 64, T * TT:(T + 1) * TT], osb[:], so(f"fst{oi}"))
                        ost.free[oi] = tk
                        store_toks[oi] = tk
                    return t_o

                LOOK = 2
                sres = {}
                t_o = None
                for i in range(len(units) + LOOK):
                    if i < len(units):
                        sres[i] = emit_S(units[i])
                    if i - LOOK >= 0:
                        t_o = emit_PV(units[i - LOOK], sres.pop(i - LOOK))
                buf_free[p] = t_o
            barrier(list(store_toks.values()))

    def phase_post(l, hsrc, hdst, pre):
        with ExitStack() as es:
            A = lambda name, shape, dt: es.enter_context(nc.sbuf_tensor(uniq(name), shape, dt))
            wg, wpa, wpb, wo, t_w = pre
            bg = A("bg", [128, 16], F32)
            g_t = A("g_t", [128, KC], F32)
            b_t = A("b_t", [128, KC], F32)
            hf = [A(f"hf{i}", [128, KC, TT], F32) for i in range(2)]
            hb = A("hb", [128, KC, TT], BF16)
            oa = [A(f"oa{i}", [128, 4, TT], BF16) for i in range(2)]
            obt = [A(f"ob{i}", [128, 4, TT], BF16) for i in range(2)]
            sg = Ring([A(f"sg{i}", [128, TT], F32) for i in range(4)])
            mg = A("mg", [128, KC, TT], BF16)
            r = A("r", [128, KC, TT], F32)
            sq = Ring([A(f"sq{i}", [128, TT], F32) for i in range(4)])
            outf = A("outf", [128, KC, TT], F32)
            lnb = [A(f"ln{i}", [128, TT], F32) for i in range(5)]
            tmp2 = [A(f"lt{i}", [128, TT], F32) for i in range(2)]
            banks = Ring([es.enter_context(nc.psum_tensor(uniq(f"p4b{i}"), [128, 512], F32)) for i in range(6)])
            ps_sum = es.enter_context(nc.psum_tensor(uniq("p4s"), [128, 512], F32))
            ps_sq = es.enter_context(nc.psum_tensor(uniq("p4q"), [128, 512], F32))

            so_b = so("p4bias")
            SP.dma(bg[:], b_g[l], so_b)
            SP.dma(g_t[:], ln1g_f[l], so_b)
            t_b = SP.dma(b_t[:], ln1b_f[l], so_b)
            ACT.wait(t_b)
            PE.wait(t_w)
            hsrc_v = hsrc.rearrange("(c p) t -> p c t", p=128)
            oat_v = OAT.rearrange("(c p) t -> p c t", p=128)
            obt_v = OBT.rearrange("(c p) t -> p c t", p=128)
            hdst_v = hdst.rearrange("(c p) t -> p c t", p=128)
            ld_sem = [so("p4ld0"), so("p4ld1")]
            ld_tok = [None, None]
            in_free = [None, None]

            def load(T):
                p = T % 2
                SP.wait(in_free[p])
                tsl = slice(T * TT, (T + 1) * TT)
                SP.dma(hf[p][:], hsrc_v[:, :, tsl], ld_sem[p])
                SP.dma(oa[p][:], oat_v[:, :, tsl], ld_sem[p])
                ld_tok[p] = SP.dma(obt[p][:], obt_v[:, :, tsl], ld_sem[p])

            load(0)
            hb_free = None
            mg_free = None
            cast_tok = None
            LN = LNState((lnb[0], lnb[1], lnb[2], lnb[3], lnb[4], tmp2), r, outf, g_t, b_t, hdst_v, "p4st")
            stats_chunk, run_deferred = make_stats(r, sq, A("acc_r", [128, TT], F32), A("acc_q", [128, TT], F32), ps_sum, ps_sq, LN)

            def emit_cast4(T_):
                p_ = T_ % 2
                ACT.wait(ld_tok[p_], hb_free)
                nc.scalar.activation(out=hb[:, 0:4, :], in_=hf[p_][:, 0:4, :], func=AF.Identity)
                return ACT.sig(nc.scalar.activation(out=hb[:, 4:8, :], in_=hf[p_][:, 4:8, :], func=AF.Identity))

            r_free = None
            out_tok = None
            for T in range(NT):
                p = T % 2
                tsl = slice(T * TT, (T + 1) * TT)
                if T + 1 < NT:
                    load(T + 1)
                if T == 0:
                    cast_tok = emit_cast4(0)
                PE.wait(cast_tok, ld_tok[p])
                t_mg_last = None
                for m in range(KC):
                    msl = slice(m * 128, (m + 1) * 128)
                    res = {}
                    for name in ("ga", "gb", "ya", "yb"):
                        bi, bank, bfree = banks.get()
                        PE.wait(bfree)
                        if name in ("ga", "gb"):
                            off = 0 if name == "ga" else 1024
                            for k in range(KC):
                                ins = nc.tensor.matmul(bank[:], lhsT=wg[:, k, off + m * 128:off + (m + 1) * 128], rhs=hb[:, k, :], start=(k == 0), stop=(k == KC - 1))
                        else:
                            wsrc, osrc = (wpa, oa[p]) if name == "ya" else (wpb, obt[p])
                            for k in range(4):
                                ins = nc.tensor.matmul(bank[:], lhsT=wsrc[:, k, msl], rhs=osrc[:, k, :], start=(k == 0), stop=(k == 3))
                        res[name] = (bi, bank, PE.sig(ins))
                    if m == 0:
                        run_deferred()
                    sgs = {}
                    for name in ("ga", "gb"):
                        bi, bank, t_mm = res[name]
                        gi, gbuf, gfree = sg.get()
                        ACT.wait(t_mm, gfree)
                        bcol = m if name == "ga" else 8 + m
                        t_sg = ACT.sig(nc.scalar.activation(out=gbuf[:], in_=bank[:], func=AF.Sigmoid, bias=bg[:, bcol:bcol + 1], scale=1.0))
                        banks.free[bi] = t_sg
                        sgs[name] = (gi, gbuf, t_sg)
                    if m > 0:
                        LN.apply_post(m - 1)
                    (gia, gba, tsa), (gib, gbb, tsb) = sgs["ga"], sgs["gb"]
                    bia, banka, tya = res["ya"]
                    bib, bankb, tyb = res["yb"]
                    DVE.wait(tsa, tya, tsb, tyb)
                    t1 = DVE.sig(nc.vector.tensor_tensor(out=gba[:], in0=gba[:], in1=banka[:], op=ALU.mult))
                    banks.free[bia] = t1
                    t2 = DVE.sig(nc.vector.tensor_tensor(out=gbb[:], in0=gbb[:], in1=bankb[:], op=ALU.mult))
                    banks.free[bib] = t2
                    DVE.wait(t2, mg_free)
                    t_mg = DVE.sig(nc.vector.tensor_tensor(out=mg[:, m, :], in0=gba[:], in1=gbb[:], op=ALU.add))
                    sg.free[gia] = t_mg
                    sg.free[gib] = t_mg
                    t_mg_last = t_mg
                    ta_ = LN.apply_pre(m)
                    if ta_ is not None:
                        r_free = ta_
                LN.apply_post(KC - 1)
                hb_free = res["gb"][2]
                in_free_tok_pe = res["yb"][2]
                if T + 1 < NT:
                    cast_tok = emit_cast4(T + 1)
                PE.wait(t_mg_last)
                DVE.wait(r_free)
                for m in range(KC):
                    msl = slice(m * 128, (m + 1) * 128)
                    bi, bank, bfree = banks.get()
                    PE.wait(bfree)
                    for k in range(KC):
                        ins = nc.tensor.matmul(bank[:], lhsT=wo[:, k, msl], rhs=mg[:, k, :], start=(k == 0), stop=(k == KC - 1))
                    t_mm = PE.sig(ins)
                    DVE.wait(t_mm)
                    t_r = DVE.sig(nc.vector.scalar_tensor_tensor(out=r[:, m, :], in0=hf[p][:, m, :], scalar=ALPHA, in1=bank[:], op0=ALU.mult, op1=ALU.add))
                    banks.free[bi] = t_r
                    stats_chunk(m, t_r, tsl)
                mg_free = t_mm
                in_free[p] = [t_r, in_free_tok_pe]
            run_deferred()
            barrier(LN.flush())

    store_tok_holder = [None]

    def phase_ffa(l, hsrc, after_loads=None):
        with ExitStack() as es:
            A = lambda name, shape, dt: es.enter_context(nc.sbuf_tensor(uniq(name), shape, dt))
            wf = A("wf", [128, KC, 2 * DFF], BF16)
            cwt = A("cwt", [128, 3, FC], F32)
            cbt = A("cbt", [128, FC], F32)
            halo = [A(f"halo{i}", [128, FC, 2], F32) for i in range(2)]
            hf = [A(f"hf{i}", [128, KC, TT], F32) for i in range(2)]
            hb = [A(f"hb{i}", [128, KC, TT], BF16) for i in range(2)]
            gb = Ring([A(f"gb{i}", [128, TT + 2], F32) for i in range(3)])
            t1r = Ring([A(f"t1{i}", [128, TT], F32) for i in range(4)])
            ast = Ring([A(f"as{i}", [128, TT], BF16) for i in range(4)])
            banks = Ring([es.enter_context(nc.psum_tensor(uniq(f"p5b{i}"), [128, 512], F32)) for i in range(7)])
            t_wc = load_w_cast(wf, w_fi[l], KC, 2 * DFF, "w5s", col_split=1408, order=[0, 2, 1, 3])
            if after_loads is not None:
                after_loads()
            so_b = so("p5bias")
            SP.dma(cwt[:], cw_f[l], so_b)
            t_b = SP.dma(cbt[:], cb_f[l], so_b)
            nc.vector.memset(halo[0][:], 0.0)
            t_h0 = DVE.sig(nc.vector.memset(halo[1][:], 0.0))
            DVE.wait(t_b)
            POOL.wait(t_b)
            ACT.wait(t_h0, t_b)
            hsrc_v = hsrc.rearrange("(c p) t -> p c t", p=128)
            actt_v = ACTT.rearrange("(f p) t -> p f t", p=128)
            ld_sem = [so("p5ld0"), so("p5ld1")]
            ld_tok = [None, None]
            hf_free = [None, None]
            hb_free = [None, None]
            cast_tok = [None, None]
            ld_tok[0] = SP.dma(hf[0][:], hsrc_v[:, :, 0:TT], ld_sem[0])
            store_toks = {}

            def emit_cast(T_):
                p_ = T_ % 2
                ACT.wait(ld_tok[p_], hb_free[p_])
                nc.scalar.activation(out=hb[p_][:, 0:4, :], in_=hf[p_][:, 0:4, :], func=AF.Identity)
                tk_ = ACT.sig(nc.scalar.activation(out=hb[p_][:, 4:8, :], in_=hf[p_][:, 4:8, :], func=AF.Identity))
                hf_free[p_] = tk_
                return tk_

            def stage2(item):
                ti, t1b, t_3 = item[0], item[1], item[2]
                ACT.wait(t_3)
                item.append(ACT.sig(nc.scalar.activation(out=t1b[:], in_=t1b[:], func=AF.Silu)))

            def stage3(item):
                ti, t1b, t_3, bi2, bank_u, t_u, f_, tsl_, t_s = item
                ai, abuf, afree = ast.get()
                DVE.wait(t_s, t_u, afree)
                t_a = DVE.sig(nc.vector.tensor_tensor(out=abuf[:], in0=t1b[:], in1=bank_u[:], op=ALU.mult))
                banks.free[bi2] = t_a
                t1r.free[ti] = t_a
                SP.wait(t_a)
                tk = SP.dma(actt_v[:, f_, tsl_], abuf[:], so(f"p5st{ai}"))
                ast.free[ai] = tk
                store_toks[ai] = tk

            pend = []
            for T in range(NT):
                p = T % 2
                tsl = slice(T * TT, (T + 1) * TT)
                hin, hout = halo[T % 2], halo[(T + 1) % 2]
                if T + 1 < NT:
                    SP.wait(hf_free[1 - p])
                    ld_tok[1 - p] = SP.dma(hf[1 - p][:], hsrc_v[:, :, (T + 1) * TT:(T + 2) * TT], ld_sem[1 - p])
                if T == 0:
                    cast_tok[0] = emit_cast(0)
                PE.wait(cast_tok[p])
                for f in range(FC):
                    bi, bank_g, bfree = banks.get()
                    PE.wait(bfree, t_wc[(f * 128) // 1408], t_wc[(DFF + f * 128) // 1408])
                    for k in range(KC):
                        ins = nc.tensor.matmul(bank_g[:], lhsT=wf[:, k, f * 128:(f + 1) * 128], rhs=hb[p][:, k, :], start=(k == 0), stop=(k == KC - 1))
                    t_g = PE.sig(ins)
                    bi2, bank_u, bfree2 = banks.get()
                    PE.wait(bfree2)
                    for k in range(KC):
                        ins = nc.tensor.matmul(bank_u[:], lhsT=wf[:, k, DFF + f * 128:DFF + (f + 1) * 128], rhs=hb[p][:, k, :], start=(k == 0), stop=(k == KC - 1))
                    t_u = PE.sig(ins)
                    if f == FC - 1:
                        hb_free[p] = t_u
                    gi, gbuf, gfree = gb.get()
                    ACT.wait(t_g, gfree)
                    nc.scalar.activation(out=gbuf[:, 0:2], in_=hin[:, f, :], func=AF.Identity)
                    nc.scalar.activation(out=gbuf[:, 2:TT + 2], in_=bank_g[:], func=AF.Identity)
                    t_ge = ACT.sig(nc.scalar.activation(out=hout[:, f, :], in_=bank_g[:, TT - 2:TT], func=AF.Identity))
                    banks.free[bi] = t_ge
                    ti, t1b, tfree = t1r.get()
                    POOL.wait(t_ge, tfree)
                    t_1 = POOL.sig(nc.gpsimd.tensor_scalar(out=t1b[:], in0=gbuf[:, 2:TT + 2], scalar1=cwt[:, 2, f:f + 1], scalar2=cbt[:, f:f + 1],
                                                           op0=ALU.mult, op1=ALU.add))
                    DVE.wait(t_1, t_ge)
                    t_2 = DVE.sig(nc.vector.scalar_tensor_tensor(out=t1b[:], in0=gbuf[:, 1:TT + 1], scalar=cwt[:, 1, f:f + 1], in1=t1b[:], op0=ALU.mult, op1=ALU.add))
                    DVE.wait(t_2)
                    t_3 = DVE.sig(nc.vector.scalar_tensor_tensor(out=t1b[:], in0=gbuf[:, 0:TT], scalar=cwt[:, 0, f:f + 1], in1=t1b[:], op0=ALU.mult, op1=ALU.add))
                    gb.free[gi] = t_3
                    item = [ti, t1b, t_3, bi2, bank_u, t_u, f, tsl]
                    if pend:
                        prev = pend.pop(0)
                        stage2(prev)
                        stage3(prev)
                    pend.append(item)
                    if f == FC // 2 and T + 1 < NT:
                        cast_tok[1 - p] = emit_cast(T + 1)
            while pend:
                prev = pend.pop(0)
                stage2(prev)
                stage3(prev)
            barrier(list(store_toks.values()))

    def phase_ffb(l, hsrc, hdst, pre):
        with ExitStack() as es:
            A = lambda name, shape, dt: es.enter_context(nc.sbuf_tensor(uniq(name), shape, dt))
            wfo, t_w = pre
            g_t = A("g_t", [128, KC], F32)
            b_t = A("b_t", [128, KC], F32)
            hf = [A(f"hf{i}", [128, KC, TT], F32) for i in range(2)]
            at = [A(f"at{i}", [128, FC, TT], BF16) for i in range(2)]
            r = A("r", [128, KC, TT], F32)
            sq = Ring([A(f"sq{i}", [128, TT], F32) for i in range(4)])
            outf = A("outf", [128, KC, TT], F32)
            lnb = [A(f"ln{i}", [128, TT], F32) for i in range(5)]
            tmp2 = [A(f"lt{i}", [128, TT], F32) for i in range(2)]
            banks = Ring([es.enter_context(nc.psum_tensor(uniq(f"p6b{i}"), [128, 512], F32)) for i in range(5)])
            ps_sum = es.enter_context(nc.psum_tensor(uniq("p6s"), [128, 512], F32))
            ps_sq = es.enter_context(nc.psum_tensor(uniq("p6q"), [128, 512], F32))
            so_b = so("p6bias")
            SP.dma(g_t[:], ln2g_f[l], so_b)
            t_b = SP.dma(b_t[:], ln2b_f[l], so_b)
            ACT.wait(t_b)
            PE.wait(t_w)
            hsrc_v = hsrc.rearrange("(c p) t -> p c t", p=128)
            hdst_v = hdst.rearrange("(c p) t -> p c t", p=128)
            actt_v = ACTT.rearrange("(f p) t -> p f t", p=128)
            ld_sem = [so("p6ld0"), so("p6ld1")]
            ld_tok = [None, None]
            in_free = [None, None]

            def load(T):
                p = T % 2
                SP.wait(in_free[p])
                tsl = slice(T * TT, (T + 1) * TT)
                SP.dma(hf[p][:], hsrc_v[:, :, tsl], ld_sem[p])
                ld_tok[p] = SP.dma(at[p][:], actt_v[:, :, tsl], ld_sem[p])

            load(0)
            LN = LNState((lnb[0], lnb[1], lnb[2], lnb[3], lnb[4], tmp2), r, outf, g_t, b_t, hdst_v, "p6st")
            stats_chunk, run_deferred = make_stats(r, sq, A("acc_r", [128, TT], F32), A("acc_q", [128, TT], F32), ps_sum, ps_sq, LN)
            for T in range(NT):
                p = T % 2
                tsl = slice(T * TT, (T + 1) * TT)
                if T + 1 < NT:
                    load(T + 1)
                PE.wait(ld_tok[p])
                DVE.wait(ld_tok[p])
                for m in range(KC):
                    msl = slice(m * 128, (m + 1) * 128)
                    bi, bank, bfree = banks.get()
                    PE.wait(bfree)
                    for f in range(FC):
                        ins = nc.tensor.matmul(bank[:], lhsT=wfo[:, f, msl], rhs=at[p][:, f, :], start=(f == 0), stop=(f == FC - 1))
                    t_mm = PE.sig(ins)
                    if m == 0:
                        run_deferred()
                    ta_ = LN.apply(m)
                    DVE.wait(t_mm, ta_)
                    t_r = DVE.sig(nc.vector.scalar_tensor_tensor(out=r[:, m, :], in0=hf[p][:, m, :], scalar=ALPHA, in1=bank[:], op0=ALU.mult, op1=ALU.add))
                    banks.free[bi] = t_r
                    stats_chunk(m, t_r, tsl)
                in_free[p] = [t_r, t_mm]
            run_deferred()
            barrier(LN.flush())

    hcur = xT
    for l in range(n_layers):
        last = (l == n_layers - 1)
        ph = 6 * l
        if ph < stop_phase:
            phase1(l, hcur)
        if ph + 1 < stop_phase:
            phase_swa(l)
        with ExitStack() as ws:
            wg = ws.enter_context(nc.sbuf_tensor(uniq("wg"), [128, KC, 2048], BF16))
            wpa = ws.enter_context(nc.sbuf_tensor(uniq("wpa"), [128, 4, D], BF16))
            wpb = ws.enter_context(nc.sbuf_tensor(uniq("wpb"), [128, 4, D], BF16))
            wo = ws.enter_context(nc.sbuf_tensor(uniq("wo"), [128, KC, D], BF16))
            tw = [None]

            def load_post():
                load_w_cast(wg, w_in[l, :, O_GA:O_GA + 2048], KC, 2048, "w4s", col_split=1024)
                load_w_cast(wpa, w_pa[l], 4, D, "w4s", col_split=1024)
                load_w_cast(wpb, w_pb[l], 4, D, "w4s", col_split=1024)
                tw[0] = load_w_cast(wo, w_o[l], KC, D, "w4s", col_split=1024)

            if ph + 2 < stop_phase:
                phase_fox(l, after_loads=load_post)
            if ph + 3 < stop_phase:
                if tw[0] is None:
                    load_post()
                phase_post(l, hcur, H1T, (wg, wpa, wpb, wo, tw[0]))
        with ExitStack() as ws:
            wfo = ws.enter_context(nc.sbuf_tensor(uniq("wfo"), [128, FC, D], BF16))
            tw = [None]

            def load_ffo():
                tw[0] = load_w_cast(wfo, w_fo[l], FC, D, "w6s", col_split=1024)

            if ph + 4 < stop_phase:
                phase_ffa(l, H1T, after_loads=load_ffo)
            if ph + 5 < stop_phase:
                if tw[0] is None:
                    load_ffo()
                phase_ffb(l, H1T, outT if last else H2T, (wfo, tw[0]))
        hcur = H2T
    top.close()
    return nc


def make_consts():
    ident = np.eye(128, dtype=np.float32)
    k = np.arange(128)[:, None]
    q = np.arange(128)[None, :]
    negm = np.where(k <= q, 0.0, -30000.0).astype(np.float32)
    slopes = (2.0 ** (-8.0 * np.arange(1, 9) / 8)).astype(np.float64)
    swam = np.zeros((128, 2, 2, 512), np.float32)
    for g in range(2):
        for hh in range(4):
            s = slopes[g * 4 + hh]
            dist_prev = (q + 128 - k).astype(np.float64)
            mp = np.where(dist_prev < 128, -8.0 * s * dist_prev, -30000.0)
            dist_cur = (q - k).astype(np.float64)
            mc = np.where(dist_cur >= 0, -8.0 * s * dist_cur, -30000.0)
            swam[:, g, 0, hh * 128:(hh + 1) * 128] = mp
            swam[:, g, 1, hh * 128:(hh + 1) * 128] = mc
    return ident, negm, swam


def host_layout(inputs):
    f = lambda a: np.ascontiguousarray(np.asarray(a, dtype=np.float32))
    b_in = f(inputs["b_in"])
    L = b_in.shape[0]
    fm = lambda v: np.ascontiguousarray(v.reshape(L, -1, 128).transpose(0, 2, 1))
    bq = np.concatenate([fm(b_in[:, O_QA:O_QA + 512]), fm(b_in[:, O_KA:O_KA + 128]), fm(b_in[:, O_QB:O_QB + 512]),
                         fm(b_in[:, O_KB:O_KB + 512])], axis=2)
    out = dict(
        b_q=f(bq), b_g=fm(b_in[:, O_GA:O_GA + 2048]), b_f=f(b_in[:, O_F:O_F + 8].reshape(L, 8, 1)),
        b_v=f(np.concatenate([b_in[:, O_VA:O_VA + 128], b_in[:, O_VB:O_VB + 512]], axis=1).reshape(L, 1, 640)),
        ln1g_f=fm(f(inputs["ln_mix_g"])), ln1b_f=fm(f(inputs["ln_mix_b"])),
        ln2g_f=fm(f(inputs["ln_ffn_g"])), ln2b_f=fm(f(inputs["ln_ffn_b"])),
        cw_f=f(f(inputs["conv_w"]).reshape(L, 3, FC, 128).transpose(0, 3, 1, 2)),
        cb_f=fm(f(inputs["conv_b"])),
    )
    return out


_WNAMES = ["w_in", "b_in", "attn_sinks", "w_proj_a", "w_proj_b", "w_out", "ln_mix_g", "ln_mix_b",
           "ln_ffn_g", "ln_ffn_b", "w_ffn_in", "conv_w", "conv_b", "w_ffn_out"]


def kernel(**inputs):
    x = np.asarray(inputs["x"], dtype=np.float32)
    ident, negm, swam = make_consts()
    nc = build_program()
    common = {n: np.ascontiguousarray(np.asarray(inputs[n], dtype=np.float32)) for n in _WNAMES}
    common.update(c_ident=ident, c_negm=negm, c_swam=swam)
    common.update(host_layout(inputs))
    in_maps = []
    for b in range(8):
        m = dict(common)
        m["xT"] = np.ascontiguousarray(x[b].T)
        in_maps.append(m)
    res = run_bass_kernel_spmd(nc, in_maps, core_ids=list(range(8)))
    out = np.stack([np.ascontiguousarray(res.results[b]["outT"].T) for b in range(8)], axis=0)
    return out.astype(np.float32)
```

```python
from contextlib import ExitStack
import numpy as np
import concourse.bass as bass
import concourse.mybir as mybir
from concourse.bass_utils import run_bass_kernel_spmd

F32 = mybir.dt.float32
BF16 = mybir.dt.bfloat16
AF = mybir.ActivationFunctionType
ALU = mybir.AluOpType

D = 1024
S = 8192
DEPTH = 2
TT = 512
NT = S // TT
NB = S // 128
KC = 8
DFF = 2816
FC = DFF // 128
NIN = 4360
ALPHA = float((2 * DEPTH) ** 0.25)
EPS = 1e-5
O_QA, O_KA, O_VA, O_QB, O_KB, O_VB, O_F, O_GA, O_GB = 0, 512, 640, 768, 1280, 1792, 2304, 2312, 3336


class SemObj:
    def __init__(self, nc, name):
        self.cm = nc.semaphore(name)
        self.sem = self.cm.__enter__()
        self.cnt = 0


class Eng:
    def __init__(self, nc, eng, name):
        self.e = eng
        self.s = SemObj(nc, "s_" + name)
        self.waited = {}

    def wait(self, *toks):
        for tok in toks:
            if tok is None:
                continue
            if isinstance(tok, list):
                self.wait(*tok)
                continue
            so, v = tok
            if self.waited.get(id(so), 0) >= v:
                continue
            self.waited[id(so)] = v
            self.e.wait_ge(so.sem, v)

    def sig(self, ins):
        self.s.cnt += 1
        ins.then_inc(self.s.sem, 1)
        return (self.s, self.s.cnt)

    def dma(self, out, in_, so, **kw):
        so.cnt += 16
        self.e.dma_start(out=out, in_=in_, **kw).then_inc(so.sem, 16)
        return (so, so.cnt)


class Ring:
    def __init__(self, bufs):
        self.bufs = bufs
        self.free = [None] * len(bufs)
        self.i = 0

    def get(self):
        idx = self.i % len(self.bufs)
        self.i += 1
        return idx, self.bufs[idx], self.free[idx]


def build_program(dbg=False, n_layers=DEPTH, stop_phase=99):
    nc = bass.Bass("TRN2", target_bir_lowering=False)

    def din(name, shape, dt=F32):
        return nc.dram_tensor(name, list(shape), dt, kind="ExternalInput").ap()

    def dscr(name, shape, dt, out=False):
        return nc.dram_tensor(name, list(shape), dt, kind="ExternalOutput" if (out and dbg) else "Internal").ap()

    xT = din("xT", [D, S])
    w_in = din("w_in", [DEPTH, D, NIN])
    b_in = din("b_in", [DEPTH, NIN])
    sinks = din("attn_sinks", [DEPTH, 8])
    w_pa = din("w_proj_a", [DEPTH, 512, D])
    w_pb = din("w_proj_b", [DEPTH, 512, D])
    w_o = din("w_out", [DEPTH, D, D])
    ln1g = din("ln_mix_g", [DEPTH, D])
    ln1b = din("ln_mix_b", [DEPTH, D])
    ln2g = din("ln_ffn_g", [DEPTH, D])
    ln2b = din("ln_ffn_b", [DEPTH, D])
    w_fi = din("w_ffn_in", [DEPTH, D, 2 * DFF])
    cw = din("conv_w", [DEPTH, 3, DFF])
    cb = din("conv_b", [DEPTH, DFF])
    w_fo = din("w_ffn_out", [DEPTH, DFF, D])
    b_q = din("b_q", [DEPTH, 128, 13])
    b_g = din("b_g", [DEPTH, 128, 16])
    b_f = din("b_f", [DEPTH, 8, 1])
    b_v = din("b_v", [DEPTH, 1, 640])
    ln1g_f = din("ln1g_f", [DEPTH, 128, KC])
    ln1b_f = din("ln1b_f", [DEPTH, 128, KC])
    ln2g_f = din("ln2g_f", [DEPTH, 128, KC])
    ln2b_f = din("ln2b_f", [DEPTH, 128, KC])
    cw_f = din("cw_f", [DEPTH, 128, 3, FC])
    cb_f = din("cb_f", [DEPTH, 128, FC])
    c_ident = din("c_ident", [128, 128])
    c_negm = din("c_negm", [128, 128])
    c_swam = din("c_swam", [128, 2, 2, 512])
    outT = nc.dram_tensor("outT", [D, S], F32, kind="ExternalOutput").ap()

    QAT = dscr("QAT", [512, S], BF16)
    KAT = dscr("KAT", [128, S], BF16)
    VA = dscr("VA", [S, 128], BF16)
    QBT = dscr("QBT", [8, 70, S], BF16)
    KBT = dscr("KBT", [8, 70, S], BF16)
    VB = dscr("VB", [S, 512], BF16)
    OAT = dscr("OAT", [512, S], BF16, out=True)
    OBT = dscr("OBT", [512, S], BF16, out=True)
    H1T = dscr("H1T", [D, S], F32, out=True)
    H2T = dscr("H2T", [D, S], F32, out=True)
    ACTT = dscr("ACTT", [DFF, S], BF16, out=True)
    CDBG = dscr("CDBG", [8, S], F32, out=True)

    PE = Eng(nc, nc.tensor, "pe")
    ACT = Eng(nc, nc.scalar, "act")
    DVE = Eng(nc, nc.vector, "dve")
    POOL = Eng(nc, nc.gpsimd, "pool")
    SP = Eng(nc, nc.sync, "sp")
    ENGS = [PE, ACT, DVE, POOL, SP]
    sems = {}
    ucnt = [0]

    def uniq(name):
        ucnt[0] += 1
        return f"{name}_{ucnt[0]}"

    def so(name):
        if name not in sems:
            sems[name] = SemObj(nc, name)
        return sems[name]

    top = ExitStack()
    identf = top.enter_context(nc.sbuf_tensor("identf", [128, 128], F32))
    identb = top.enter_context(nc.sbuf_tensor("identb", [128, 128], BF16))
    negmb = top.enter_context(nc.sbuf_tensor("negmb", [128, 128], BF16))
    onesf = top.enter_context(nc.sbuf_tensor("onesf", [128, 128], F32))
    tiny = top.enter_context(nc.sbuf_tensor("tiny", [128, 8], F32))

    t0 = SP.dma(identf[:], c_ident, so("c0"))
    t1 = POOL.dma(identb[:], c_ident, so("c1"))
    t2 = POOL.dma(negmb[:], c_negm, so("c2"))
    DVE.wait(t0, t1, t2)
    nc.vector.memset(onesf[:], 1.0)
    t_const = DVE.sig(nc.vector.memset(tiny[:], 0.0))
    for E in ENGS:
        E.wait(t_const)

    def barrier(extra=()):
        toks = list(extra)
        toks.append(ACT.sig(nc.scalar.activation(out=tiny[0:1, 0:1], in_=tiny[0:1, 4:5], func=AF.Identity)))
        toks.append(DVE.sig(nc.vector.memset(tiny[0:1, 1:2], 0.0)))
        toks.append(POOL.sig(nc.gpsimd.memset(tiny[0:1, 2:3], 0.0)))
        for E in ENGS:
            E.wait(*toks)

    def load_w_cast(dst3, src2, nchunk, ncols, semname, col_split=2048, order=None):
        toks = []
        srcv = src2.rearrange("(c p) n -> p c n", p=128)
        starts = list(range(0, ncols, col_split))
        if order is None:
            for c0 in starts:
                c1 = min(ncols, c0 + col_split)
                toks.append(POOL.dma(dst3[:, :, c0:c1], srcv[:, :, c0:c1], so(semname)))
            return toks[-1]
        out = {}
        for ci in order:
            c0 = starts[ci]
            c1 = min(ncols, c0 + col_split)
            out[ci] = POOL.dma(dst3[:, :, c0:c1], srcv[:, :, c0:c1], so(f"{semname}_{ci}"))
        return out

    class LNState:
        def __init__(self, lnbufs, r, outf, g_t, b_t, hdst_v, sempfx):
            self.mean_s, self.msq, self.var, self.rstd, self.nmr, self.tmp = lnbufs
            self.r, self.outf, self.g_t, self.b_t, self.hdst_v, self.sempfx = r, outf, g_t, b_t, hdst_v, sempfx
            self.pending = None
            self.tmp_free = [None, None]
            self.st_tok = [None] * KC
            self.pre_tok = [None] * KC
            self.last_dve = None
            self.last_pool = None
            self.stats_free = None
            self.t_nmr = None

        def finalize(self, tsl, ps_sum, ps_sq, t_stats):
            assert self.pending is None
            ACT.wait(t_stats)
            t_mean = ACT.sig(nc.scalar.activation(out=self.mean_s[:], in_=ps_sum[:], func=AF.Identity, scale=1.0 / D))
            DVE.wait(t_mean, t_stats)
            t_msq = DVE.sig(nc.vector.tensor_tensor(out=self.msq[:], in0=self.mean_s[:], in1=self.mean_s[:], op=ALU.mult))
            DVE.wait(t_msq)
            t_var = DVE.sig(nc.vector.scalar_tensor_tensor(out=self.var[:], in0=ps_sq[:], scalar=1.0 / D, in1=self.msq[:], op0=ALU.mult, op1=ALU.subtract))
            ACT.wait(t_var)
            t_ln = ACT.sig(nc.scalar.activation(out=self.var[:], in_=self.var[:], func=AF.Ln, bias=eps_t[:, 0:1], scale=1.0))
            ACT.wait(t_ln, self.last_dve)
            t_rstd = ACT.sig(nc.scalar.activation(out=self.rstd[:], in_=self.var[:], func=AF.Exp, scale=-0.5))
            self.stats_free = t_rstd
            DVE.wait(t_rstd, self.last_pool)
            self.t_nmr = DVE.sig(nc.vector.scalar_tensor_tensor(out=self.nmr[:], in0=self.mean_s[:], scalar=-1.0, in1=self.rstd[:], op0=ALU.mult, op1=ALU.mult))
            self.pending = tsl

        def apply_pre(self, m):
            if self.pending is None:
                return None
            tb = self.tmp[m % 2]
            DVE.wait(self.tmp_free[m % 2], self.t_nmr)
            t_a = DVE.sig(nc.vector.tensor_tensor(out=tb[:], in0=self.r[:, m, :], in1=self.rstd[:], op=ALU.mult))
            POOL.wait(t_a, self.t_nmr)
            t_t = POOL.sig(nc.gpsimd.tensor_tensor(out=tb[:], in0=tb[:], in1=self.nmr[:], op=ALU.add))
            self.pre_tok[m] = t_t
            self.last_dve, self.last_pool = t_a, t_t
            return t_a

        def apply_post(self, m):
            if self.pending is None or self.pre_tok[m] is None:
                return
            tsl = self.pending
            tb = self.tmp[m % 2]
            ACT.wait(self.pre_tok[m], self.st_tok[m])
            t_o = ACT.sig(nc.scalar.activation(out=self.outf[:, m, :], in_=tb[:], func=AF.Identity,
                                               scale=self.g_t[:, m:m + 1], bias=self.b_t[:, m:m + 1]))
            self.tmp_free[m % 2] = t_o
            self.pre_tok[m] = None
            SP.wait(t_o)
            self.st_tok[m] = SP.dma(self.hdst_v[:, m, tsl], self.outf[:, m, :], so(f"{self.sempfx}{m}"))
            if m == KC - 1:
                self.pending = None

        def apply(self, m):
            t_a = self.apply_pre(m)
            self.apply_post(m)
            return t_a

        def flush(self):
            for m in range(KC):
                self.apply(m)
            return [t for t in self.st_tok if t is not None]

    def make_stats(r, sq, acc_r, acc_q, ps_sum, ps_sq, LN):
        st = {"ar": None, "aq": None, "pe": None, "deferred": None}

        def stats_chunk(m, t_r, tsl):
            qi, sqb, qfree = sq.get()
            ACT.wait(t_r, qfree)
            t_q = ACT.sig(nc.scalar.activation(out=sqb[:], in_=r[:, m, :], func=AF.Square))
            if m == 0:
                DVE.wait(t_r, st["pe"])
                st["ar"] = DVE.sig(nc.vector.tensor_copy(out=acc_r[:], in_=r[:, 0, :]))
                POOL.wait(t_q, st["pe"])
                st["aq"] = POOL.sig(nc.gpsimd.tensor_copy(out=acc_q[:], in_=sqb[:]))
            else:
                DVE.wait(t_r, st["ar"])
                st["ar"] = DVE.sig(nc.vector.tensor_tensor(out=acc_r[:], in0=acc_r[:], in1=r[:, m, :], op=ALU.add))
                if m == KC - 1:
                    DVE.wait(t_q, st["aq"])
                    st["aq"] = DVE.sig(nc.vector.tensor_tensor(out=acc_q[:], in0=acc_q[:], in1=sqb[:], op=ALU.add))
                else:
                    POOL.wait(t_q, st["aq"])
                    st["aq"] = POOL.sig(nc.gpsimd.tensor_tensor(out=acc_q[:], in0=acc_q[:], in1=sqb[:], op=ALU.add))
            sq.free[qi] = st["aq"]
            if m == KC - 1:
                t_ar, t_aq = st["ar"], st["aq"]

                def deferred():
                    PE.wait(t_ar, t_aq, LN.stats_free)
                    nc.tensor.matmul(ps_sum[:], lhsT=onesf[:], rhs=acc_r[:], start=True, stop=True)
                    st["pe"] = PE.sig(nc.tensor.matmul(ps_sq[:], lhsT=onesf[:], rhs=acc_q[:], start=True, stop=True))
                    LN.finalize(tsl, ps_sum, ps_sq, st["pe"])

                st["deferred"] = deferred

        def run_deferred():
            if st["deferred"] is not None:
                st["deferred"]()
                st["deferred"] = None

        return stats_chunk, run_deferred

    eps_t = top.enter_context(nc.sbuf_tensor("eps_t", [128, 1], F32))
    t_eps = DVE.sig(nc.vector.memset(eps_t[:], EPS))
    ACT.wait(t_eps)

    def phase1(l, hsrc):
        with ExitStack() as es:
            A = lambda name, shape, dt: es.enter_context(nc.sbuf_tensor(uniq(name), shape, dt))
            w1 = A("w1", [128, KC, 2312], BF16)
            bq = A("bq", [128, 13], F32)
            bfn = A("bfn", [8, 1], F32)
            bv = A("bv", [128, 640], F32)
            hf = [A(f"hf{i}", [128, KC, TT], F32) for i in range(2)]
            hb = [A(f"hb{i}", [128, KC, TT], BF16) for i in range(2)]
            st = Ring([A(f"st{i}", [128, TT], BF16) for i in range(4)])
            sv = Ring([A(f"sv{i}", [128, 4, 640], BF16) for i in range(2)])
            cc = A("cc", [8, S], F32)
            et = A("et", [8, TT], F32)
            lt = A("lt", [8, TT], F32)
            shb = Ring([(A(f"kp{i}", [8, 3, TT], BF16), A(f"qp{i}", [8, 3, TT], BF16), A(f"r1_{i}", [8, TT], F32), A(f"r2_{i}", [8, TT], F32)) for i in range(2)])
            ones8 = A("ones8", [8, 3, TT], BF16)
            eights = A("eights", [8, 3, TT], BF16)
            banks = Ring([es.enter_context(nc.psum_tensor(uniq(f"p1b{i}"), [128, 512], F32)) for i in range(6)])

            t_w = load_w_cast(w1, w_in[l, :, 0:2312], KC, 2312, "w1s", col_split=1156)
            so_b = so("p1bias")
            SP.dma(bq[:], b_q[l], so_b)
            SP.dma(bfn[:], b_f[l], so_b)
            t_b = SP.dma(bv[:], b_v[l].partition_broadcast(128), so_b)
            DVE.wait(t_b)
            nc.vector.tensor_scalar(out=bfn[:], in0=bfn[:], scalar1=-1.0, scalar2=None, op0=ALU.mult)
            nc.vector.memset(eights[:], 8.0)
            t_init = DVE.sig(nc.vector.memset(ones8[:], 1.0))
            ACT.wait(t_b, t_init)
            SP.wait(t_init)
            PE.wait(t_w)

            hsrc_v = hsrc.rearrange("(c p) t -> p c t", p=128)
            ld_sem = [so("p1ld0"), so("p1ld1")]
            hf_free = [None, None]
            hb_free = [None, None]
            ld_tok = [None, None]
            ld_tok[0] = SP.dma(hf[0][:], hsrc_v[:, :, 0:TT], ld_sem[0])
            store_toks = {}
            out_specs = []
            for c in range(4):
                out_specs.append((O_QA + c * 128, ("QA", c)))
            out_specs.append((O_KA, ("KA", 0)))
            for c in range(4):
                out_specs.append((O_QB + c * 128, ("QB", c)))
            for c in range(4):
                out_specs.append((O_KB + c * 128, ("KB", c)))
            prev_c_last = None
            cast_tok = [None, None]

            def emit_cast(T_):
                p_ = T_ % 2
                ACT.wait(ld_tok[p_], hb_free[p_])
                nc.scalar.activation(out=hb[p_][:, 0:4, :], in_=hf[p_][:, 0:4, :], func=AF.Identity)
                tk_ = ACT.sig(nc.scalar.activation(out=hb[p_][:, 4:8, :], in_=hf[p_][:, 4:8, :], func=AF.Identity))
                hf_free[p_] = tk_
                return tk_

            for T in range(NT):
                p = T % 2
                tsl = slice(T * TT, (T + 1) * TT)
                if T + 1 < NT:
                    SP.wait(hf_free[1 - p])
                    ld_tok[1 - p] = SP.dma(hf[1 - p][:], hsrc_v[:, :, (T + 1) * TT:(T + 2) * TT], ld_sem[1 - p])
                if T == 0:
                    cast_tok[0] = emit_cast(0)
                PE.wait(cast_tok[p])
                bi, bank, bfree = banks.get()
                PE.wait(bfree)
                for k in range(KC):
                    ins = nc.tensor.matmul(bank[0:8, :], lhsT=w1[:, k, O_F:O_F + 8], rhs=hb[p][:, k, :], start=(k == 0), stop=(k == KC - 1))
                t_mm = PE.sig(ins)
                ACT.wait(t_mm, prev_c_last)
                t_e = ACT.sig(nc.scalar.activation(out=et[:], in_=bank[0:8, :], func=AF.Exp, scale=-1.0, bias=bfn[:, 0:1]))
                ACT.wait(t_e)
                t_lf = ACT.sig(nc.scalar.activation(out=lt[:], in_=et[:], func=AF.Ln, bias=1.0, scale=1.0))
                banks.free[bi] = t_lf
                DVE.wait(t_lf)
                init = 0.0 if T == 0 else cc[:, T * TT - 1:T * TT]
                t_sc = DVE.sig(nc.vector.tensor_tensor_scan(out=cc[:, tsl], data0=lt[:], data1=lt[:], initial=init, op0=ALU.add, op1=ALU.bypass))
                hi, (kp, qp, r1, r2), shfree = shb.get()
                DVE.wait(shfree, t_sc)
                t_a = DVE.sig(nc.vector.tensor_copy(out=kp[:, 0, :], in_=cc[:, tsl]))
                DVE.wait(t_a)
                t_b_ = DVE.sig(nc.vector.tensor_tensor(out=r1[:], in0=cc[:, tsl], in1=kp[:, 0, :], op=ALU.subtract))
                DVE.wait(t_b_)
                t_c_ = DVE.sig(nc.vector.tensor_copy(out=kp[:, 1, :], in_=r1[:]))
                DVE.wait(t_c_)
                t_d_ = DVE.sig(nc.vector.tensor_tensor(out=r2[:], in0=r1[:], in1=kp[:, 1, :], op=ALU.subtract))
                DVE.wait(t_d_)
                t_e_ = DVE.sig(nc.vector.tensor_copy(out=kp[:, 2, :], in_=r2[:]))
                DVE.wait(t_e_)
                t_sh = DVE.sig(nc.vector.tensor_scalar(out=qp[:], in0=kp[:], scalar1=-8.0, scalar2=None, op0=ALU.mult))
                prev_c_last = t_sh
                SP.wait(t_sh)
                hsem = so(f"p1sh{hi}")
                SP.dma(KBT[:, 67:70, tsl], kp[:], hsem)
                SP.dma(QBT[:, 64:67, tsl], qp[:], hsem)
                SP.dma(KBT[:, 64:67, tsl], ones8[:], hsem)
                tk = SP.dma(QBT[:, 67:70, tsl], eights[:], hsem)
                shb.free[hi] = tk
                store_toks[("sh", hi)] = tk
                for oi, (wo, (kind, c)) in enumerate(out_specs):
                    if oi == 6 and T + 1 < NT:
                        cast_tok[1 - p] = emit_cast(T + 1)
                    bi, bank, bfree = banks.get()
                    PE.wait(bfree)
                    for k in range(KC):
                        ins = nc.tensor.matmul(bank[:], lhsT=w1[:, k, wo:wo + 128], rhs=hb[p][:, k, :], start=(k == 0), stop=(k == KC - 1))
                    t_mm = PE.sig(ins)
                    si, sbuf, sfree = st.get()
                    ACT.wait(t_mm, sfree)
                    t_ev = ACT.sig(nc.scalar.activation(out=sbuf[:], in_=bank[:], func=AF.Identity, bias=bq[:, oi:oi + 1], scale=1.0))
                    banks.free[bi] = t_ev
                    SP.wait(t_ev)
                    ssem = so(f"p1st{si}")
                    if kind == "QA":
                        tk = SP.dma(QAT[c * 128:(c + 1) * 128, tsl], sbuf[:], ssem)
                    elif kind == "KA":
                        tk = SP.dma(KAT[:, tsl], sbuf[:], ssem)
                    else:
                        dst = QBT if kind == "QB" else KBT
                        SP.dma(dst[2 * c, 0:64, tsl], sbuf[0:64, :], ssem)
                        tk = SP.dma(dst[2 * c + 1, 0:64, tsl], sbuf[64:128, :], ssem)
                    st.free[si] = tk
                    store_toks[("st", si)] = tk
                vi, svbuf, svfree = sv.get()
                DVE.wait(svfree)
                for b in range(4):
                    bi, bank, bfree = banks.get()
                    PE.wait(bfree)
                    for k in range(KC):
                        ins = nc.tensor.matmul(bank[:], lhsT=hb[p][:, k, b * 128:(b + 1) * 128], rhs=w1[:, k, O_VB:O_VB + 512], start=(k == 0), stop=(k == KC - 1))
                    t_mb = PE.sig(ins)
                    bi2, bank2, bfree2 = banks.get()
                    PE.wait(bfree2)
                    for k in range(KC):
                        ins = nc.tensor.matmul(bank2[:, 0:128], lhsT=hb[p][:, k, b * 128:(b + 1) * 128], rhs=w1[:, k, O_VA:O_VA + 128], start=(k == 0), stop=(k == KC - 1))
                    t_ma = PE.sig(ins)
                    DVE.wait(t_mb, t_ma)
                    t_e1 = DVE.sig(nc.vector.tensor_tensor(out=svbuf[:, b, 128:640], in0=bank[:], in1=bv[:, 128:640], op=ALU.add))
                    t_e2 = DVE.sig(nc.vector.tensor_tensor(out=svbuf[:, b, 0:128], in0=bank2[:, 0:128], in1=bv[:, 0:128], op=ALU.add))
                    banks.free[bi] = t_e1
                    banks.free[bi2] = t_e2
                hb_free[p] = t_ma
                SP.wait(t_e2)
                vsem = so(f"p1sv{vi}")
                SP.dma(VA[tsl, :].rearrange("(b p) d -> p b d", p=128), svbuf[:, :, 0:128], vsem)
                tk = SP.dma(VB[tsl, :].rearrange("(b p) d -> p b d", p=128), svbuf[:, :, 128:640], vsem)
                sv.free[vi] = tk
                store_toks[("sv", vi)] = tk
            if dbg:
                SP.wait(prev_c_last)
                store_toks["cdbg"] = SP.dma(CDBG, cc[:], so("cdbg"))
            barrier([v for v in store_toks.values()])

    def phase_swa(l):
        with ExitStack() as es:
            A = lambda name, shape, dt: es.enter_context(nc.sbuf_tensor(uniq(name), shape, dt))
            qa = A("qa", [64, 4, S], BF16)
            ka = A("ka", [64, S], BF16)
            va = A("va", [128, NB, 128], BF16)
            mk = A("mk", [128, 2, 2, 512], BF16)
            esk = A("esk", [128, 8], F32)
            zf = A("zf", [128, 128], F32)
            esk512 = A("esk512", [128, 2, 512], F32)
            pb = Ring([A(f"pb{i}", [128, 512], BF16) for i in range(4)])
            dnr = Ring([A(f"dn{i}", [128, 512], F32) for i in range(2)])
            lnr = Ring([A(f"lnr{i}", [128, 512], F32) for i in range(2)])
            rcr = Ring([A(f"rcr{i}", [128, 512], F32) for i in range(2)])
            ost = Ring([A(f"ost{i}", [64, 4, TT], BF16) for i in range(2)])
            sb = Ring([es.enter_context(nc.psum_tensor(uniq(f"swS{i}"), [128, 512], F32)) for i in range(4)])
            ob = Ring([es.enter_context(nc.psum_tensor(uniq(f"swO{i}"), [128, 512], F32)) for i in range(3)])
            so_c = so("swc")
            t_mk = POOL.dma(mk[:], c_swam, so("swmk"))
            t_c = SP.dma(esk[:], sinks[l].rearrange("(o n) -> o n", o=1).partition_broadcast(128), so_c)
            ACT.wait(t_c)
            t_es = ACT.sig(nc.scalar.activation(out=esk[:], in_=esk[:], func=AF.Exp))
            t_ones = POOL.sig(nc.gpsimd.memset(va[:, :, 64:128], 1.0))
            t_z = DVE.sig(nc.vector.memset(zf[:], 0.0))
            DVE.wait(t_es, t_z)
            for g in range(2):
                for hh in range(4):
                    t_e5 = DVE.sig(nc.vector.tensor_scalar(out=esk512[64:128, g, hh * 128:(hh + 1) * 128], in0=zf[64:128, :],
                                                           scalar1=esk[64:128, g * 4 + hh:g * 4 + hh + 1], scalar2=None, op0=ALU.add))
            DVE.wait(t_e5)
            PE.wait(t_mk)
            store_toks = {}
            last_use = None
            chunk_free = [None] * 4
            pending_norm = [None]
            NCH = 4
            CW = S // NCH
            ch_tok = [[None] * NCH, [None] * NCH]

            def emit_load(g_, c):
                SP.wait(chunk_free[c])
                so_l = so(f"swld{c}")
                csl = slice(c * CW, (c + 1) * CW)
                SP.dma(qa[:, 0:2, csl], QAT[g_ * 256:g_ * 256 + 128, csl].rearrange("(h d) t -> d h t", d=64), so_l)
                SP.dma(qa[:, 2:4, csl], QAT[g_ * 256 + 128:g_ * 256 + 256, csl].rearrange("(h d) t -> d h t", d=64), so_l)
                SP.dma(ka[:, csl], KAT[g_ * 64:(g_ + 1) * 64, csl], so_l)
                return SP.dma(va[:, c * (NB // NCH):(c + 1) * (NB // NCH), 0:64],
                              VA[csl, g_ * 64:(g_ + 1) * 64].rearrange("(n p) d -> p n d", p=128), so_l)

            for g in range(2):
                if g == 0:
                    SP.wait(t_ones)
                    for c in range(NCH):
                        ch_tok[0][c] = emit_load(0, c)
                oi = osb = None
                def stage_a(n, g=g):
                    if n % (NB // NCH) == 0:
                        PE.wait(ch_tok[g][n // (NB // NCH)])
                    qsl = slice(n * 128, (n + 1) * 128)
                    parts = ([("prev", n - 1)] if n > 0 else []) + [("cur", n)]
                    pbs = []
                    for (which, kb) in parts:
                        midx = 0 if which == "prev" else 1
                        si, sbank, sfree = sb.get()
                        PE.wait(sfree)
                        nc.tensor.matmul(sbank[:].rearrange("p (h q) -> p h q", h=4), lhsT=ka[:, kb * 128:(kb + 1) * 128], rhs=qa[:, :, qsl], start=True, stop=False)
                        t_s = PE.sig(nc.tensor.matmul(sbank[:], lhsT=identb[:], rhs=mk[:, g, midx, :], start=False, stop=True))
                        bi_, pbb, pbfree = pb.get()
                        ACT.wait(t_s, pbfree)
                        t_p = ACT.sig(nc.scalar.activation(out=pbb[:], in_=sbank[:], func=AF.Exp, scale=0.125))
                        sb.free[si] = t_p
                        pbs.append((bi_, pbb, t_p, kb))
                    return pbs

                pbs_next = stage_a(0)
                for n in range(NB):
                    pbs = pbs_next
                    if n + 1 < NB:
                        pbs_next = stage_a(n + 1)
                    if n % 4 == 0:
                        oi, osb, ofree = ost.get()
                        DVE.wait(ofree)
                    obi, obank, obfree = ob.get()
                    PE.wait(obfree)
                    for ii, (bi_, pbb, t_p, kb) in enumerate(pbs):
                        PE.wait(t_p)
                        t_o = PE.sig(nc.tensor.matmul(obank[:], lhsT=va[:, kb, :], rhs=pbb[:], start=(ii == 0), stop=(ii == len(pbs) - 1)))
                        pb.free[bi_] = t_o
                    di, dnb, dfree = dnr.get()
                    DVE.wait(t_o, dfree)
                    t_d = DVE.sig(nc.vector.tensor_tensor(out=dnb[64:128, :], in0=obank[64:128, :], in1=esk512[64:128, g, :], op=ALU.add))

                    def norm(n=n, g=g, obi=obi, obank=obank, di=di, dnb=dnb, t_d=t_d, oi=oi, osb=osb):
                        li, lnb_, lfree = lnr.get()
                        ACT.wait(t_d, lfree)
                        t_l = ACT.sig(nc.scalar.activation(out=lnb_[64:128, :], in_=dnb[64:128, :], func=AF.Ln))
                        dnr.free[di] = t_l
                        ri, rcb, rfree = rcr.get()
                        ACT.wait(t_l, rfree)
                        t_r = ACT.sig(nc.scalar.activation(out=rcb[64:128, :], in_=lnb_[64:128, :], func=AF.Exp, scale=-1.0))
                        lnr.free[li] = t_r
                        DVE.wait(t_r)
                        t_n = DVE.sig(nc.vector.tensor_tensor(out=osb[:, :, (n % 4) * 128:(n % 4 + 1) * 128],
                                                              in0=obank[0:64, :].rearrange("p (h q) -> p h q", h=4),
                                                              in1=rcb[64:128, :].rearrange("p (h q) -> p h q", h=4), op=ALU.mult))
                        ob.free[obi] = t_n
                        rcr.free[ri] = t_n
                        if n % 4 == 3:
                            SP.wait(t_n)
                            T = n // 4
                            osem = so(f"swst{oi}")
                            SP.dma(OAT[g * 256:g * 256 + 128, T * TT:(T + 1) * TT].rearrange("(h d) t -> d h t", d=64), osb[:, 0:2, :], osem)
                            tk = SP.dma(OAT[g * 256 + 128:g * 256 + 256, T * TT:(T + 1) * TT].rearrange("(h d) t -> d h t", d=64), osb[:, 2:4, :], osem)
                            ost.free[oi] = tk
                            store_toks[oi] = tk

                    if pending_norm[0] is not None:
                        pending_norm[0]()
                    pending_norm[0] = norm
                    last_use = t_o
                    if n % (NB // NCH) == 0 and n > 0:
                        chunk_free[n // (NB // NCH) - 1] = t_o
                        if g == 0:
                            ch_tok[1][n // (NB // NCH) - 1] = emit_load(1, n // (NB // NCH) - 1)
                    if n == NB - 1:
                        chunk_free[NCH - 1] = t_o
                        if g == 0:
                            ch_tok[1][NCH - 1] = emit_load(1, NCH - 1)
                if pending_norm[0] is not None:
                    pending_norm[0]()
                    pending_norm[0] = None
            barrier(list(store_toks.values()))

    def phase_fox(l, after_loads=None):
        with ExitStack() as es:
            A = lambda name, shape, dt: es.enter_context(nc.sbuf_tensor(uniq(name), shape, dt))
            qb = [A(f"fq{i}", [70, S], BF16) for i in range(2)]
            kb = [A(f"fk{i}", [70, S], BF16) for i in range(2)]
            vb = [A(f"fv{i}", [128, NB, 128], BF16) for i in range(2)]
            pr = Ring([A(f"fp{i}", [128, 2 * TT], BF16) for i in range(3)])
            rc = Ring([A(f"frc{i}", [128, 512], F32) for i in range(2)])
            ost = Ring([A(f"fo{i}", [64, 512], BF16) for i in range(2)])
            sall = es.enter_context(nc.psum_tensor(uniq("fS"), [128, 6 * TT], F32))
            sb = Ring([sall[:, i * 2 * TT:(i + 1) * 2 * TT] for i in range(3)])
            ob = Ring([es.enter_context(nc.psum_tensor(uniq(f"fO{i}"), [128, 512], F32)) for i in range(2)])
            nc.gpsimd.memset(vb[0][:, :, 64:128], 1.0)
            t_ones = POOL.sig(nc.gpsimd.memset(vb[1][:, :, 64:128], 1.0))
            SP.wait(t_ones)
            ld_tok = [None, None]
            buf_free = [None, None]
            store_toks = {}

            def load_head(h):
                p = h % 2
                SP.wait(buf_free[p])
                sl = so(f"fld{p}")
                SP.dma(qb[p][:], QBT[h], sl)
                SP.dma(kb[p][:], KBT[h], sl)
                ld_tok[p] = SP.dma(vb[p][:, :, 0:64], VB[:, h * 64:(h + 1) * 64].rearrange("(n p) d -> p n d", p=128), sl)

            load_head(0)
            if after_loads is not None:
                after_loads()
            for h in range(8):
                p = h % 2
                if h + 1 < 8:
                    load_head(h + 1)
                PE.wait(ld_tok[p])
                Q, K, V = qb[p], kb[p], vb[p]
                units = []
                for T in range(NT):
                    nj = 4 * T + 4
                    for j in range(0, 4 * T, 2):
                        units.append((T, j, nj, True))
                    for j in range(4 * T, nj):
                        units.append((T, j, nj, False))
                state = {}

                def emit_S(u):
                    T, j, nj, pair = u
                    si, sbank, sfree = sb.get()
                    PE.wait(sfree)
                    q0 = T * TT
                    if pair:
                        nc.tensor.matmul(sbank[:, 0:TT], lhsT=K[:, j * 128:(j + 1) * 128], rhs=Q[:, q0:q0 + TT], start=True, stop=True)
                        t_s = PE.sig(nc.tensor.matmul(sbank[:, TT:2 * TT], lhsT=K[:, (j + 1) * 128:(j + 2) * 128], rhs=Q[:, q0:q0 + TT], start=True, stop=True))
                        cols = (0, 2 * TT)
                    else:
                        c0 = (j - 4 * T) * 128
                        nc.tensor.matmul(sbank[:, c0:c0 + 128], lhsT=K[:, j * 128:(j + 1) * 128], rhs=Q[:, q0 + c0:q0 + c0 + 128], start=True, stop=False)
                        t_s = PE.sig(nc.tensor.matmul(sbank[:, c0:c0 + 128], lhsT=identb[:], rhs=negmb[:], start=False, stop=True))
                        if c0 + 128 < TT:
                            t_s = PE.sig(nc.tensor.matmul(sbank[:, c0 + 128:TT], lhsT=K[:, j * 128:(j + 1) * 128], rhs=Q[:, q0 + c0 + 128:q0 + TT], start=True, stop=True))
                        cols = (c0, TT)
                    pi, pbuf, pfree = pr.get()
                    ACT.wait(t_s, pfree)
                    a_, b_ = cols
                    t_p = ACT.sig(nc.scalar.activation(out=pbuf[:, a_:b_], in_=sbank[:, a_:b_], func=AF.Exp, scale=0.125))
                    sb.free[si] = t_p
                    return (pi, pbuf, t_p, cols)

                def emit_PV(u, sres):
                    T, j, nj, pair = u
                    pi, pbuf, t_p, (a_, b_) = sres
                    if j == 0:
                        obi, obank, obfree = ob.get()
                        PE.wait(obfree)
                        state[("o", T)] = (obi, obank)
                    obi, obank = state[("o", T)]
                    PE.wait(t_p)
                    if pair:
                        nc.tensor.matmul(obank[:], lhsT=V[:, j, :], rhs=pbuf[:, 0:TT], start=(j == 0), stop=False)
                        t_o = PE.sig(nc.tensor.matmul(obank[:], lhsT=V[:, j + 1, :], rhs=pbuf[:, TT:2 * TT], start=False, stop=False))
                    else:
                        t_o = PE.sig(nc.tensor.matmul(obank[:, a_:b_], lhsT=V[:, j, :], rhs=pbuf[:, a_:b_], start=(j == 0), stop=(j == nj - 1)))
                    pr.free[pi] = t_o
                    if j == nj - 1:
                        ri, rcb, rfree = rc.get()
                        DVE.wait(t_o, rfree)
                        t_rc = DVE.sig(nc.vector.reciprocal(out=rcb[64:128, :], in_=obank[64:128, :]))
                        oi, osb, ofree = ost.get()
                        DVE.wait(ofree, t_rc)
                        t_n = DVE.sig(nc.vector.tensor_tensor(out=osb[:], in0=obank[0:64, :], in1=rcb[64:128, :], op=ALU.mult))
                        ob.free[obi] = t_n
                        rc.free[ri] = t_n
                        SP.wait(t_n)
                        tk = SP.dma(OBT[h * 64:(h + 1) * 64, T * TT:(T + 1) * TT], osb[:], so(f"fst{oi}"))
                        ost.free[oi] = tk
                        store_toks[oi] = tk
                    return t_o

                LOOK = 2
                sres = {}
                t_o = None
                for i in range(len(units) + LOOK):
                    if i < len(units):
                        sres[i] = emit_S(units[i])
                    if i - LOOK >= 0:
                        t_o = emit_PV(units[i - LOOK], sres.pop(i - LOOK))
                buf_free[p] = t_o
            barrier(list(store_toks.values()))

    def phase_post(l, hsrc, hdst, pre):
        with ExitStack() as es:
            A = lambda name, shape, dt: es.enter_context(nc.sbuf_tensor(uniq(name), shape, dt))
            wg, wpa, wpb, wo, t_w = pre
            bg = A("bg", [128, 16], F32)
            g_t = A("g_t", [128, KC], F32)
            b_t = A("b_t", [128, KC], F32)
            hf = [A(f"hf{i}", [128, KC, TT], F32) for i in range(2)]
            hb = A("hb", [128, KC, TT], BF16)
            oa = [A(f"oa{i}", [128, 4, TT], BF16) for i in range(2)]
            obt = [A(f"ob{i}", [128, 4, TT], BF16) for i in range(2)]
            sg = Ring([A(f"sg{i}", [128, TT], F32) for i in range(4)])
            mg = A("mg", [128, KC, TT], BF16)
            r = A("r", [128, KC, TT], F32)
            sq = Ring([A(f"sq{i}", [128, TT], F32) for i in range(4)])
            outf = A("outf", [128, KC, TT], F32)
            lnb = [A(f"ln{i}", [128, TT], F32) for i in range(5)]
            tmp2 = [A(f"lt{i}", [128, TT], F32) for i in range(2)]
            banks = Ring([es.enter_context(nc.psum_tensor(uniq(f"p4b{i}"), [128, 512], F32)) for i in range(6)])
            ps_sum = es.enter_context(nc.psum_tensor(uniq("p4s"), [128, 512], F32))
            ps_sq = es.enter_context(nc.psum_tensor(uniq("p4q"), [128, 512], F32))

            so_b = so("p4bias")
            SP.dma(bg[:], b_g[l], so_b)
            SP.dma(g_t[:], ln1g_f[l], so_b)
            t_b = SP.dma(b_t[:], ln1b_f[l], so_b)
            ACT.wait(t_b)
            PE.wait(t_w)
            hsrc_v = hsrc.rearrange("(c p) t -> p c t", p=128)
            oat_v = OAT.rearrange("(c p) t -> p c t", p=128)
            obt_v = OBT.rearrange("(c p) t -> p c t", p=128)
            hdst_v = hdst.rearrange("(c p) t -> p c t", p=128)
            ld_sem = [so("p4ld0"), so("p4ld1")]
            ld_tok = [None, None]
            in_free = [None, None]

            def load(T):
                p = T % 2
                SP.wait(in_free[p])
                tsl = slice(T * TT, (T + 1) * TT)
                SP.dma(hf[p][:], hsrc_v[:, :, tsl], ld_sem[p])
                SP.dma(oa[p][:], oat_v[:, :, tsl], ld_sem[p])
                ld_tok[p] = SP.dma(obt[p][:], obt_v[:, :, tsl], ld_sem[p])

            load(0)
            hb_free = None
            mg_free = None
            cast_tok = None
            LN = LNState((lnb[0], lnb[1], lnb[2], lnb[3], lnb[4], tmp2), r, outf, g_t, b_t, hdst_v, "p4st")
            stats_chunk, run_deferred = make_stats(r, sq, A("acc_r", [128, TT], F32), A("acc_q", [128, TT], F32), ps_sum, ps_sq, LN)

            def emit_cast4(T_):
                p_ = T_ % 2
                ACT.wait(ld_tok[p_], hb_free)
                nc.scalar.activation(out=hb[:, 0:4, :], in_=hf[p_][:, 0:4, :], func=AF.Identity)
                return ACT.sig(nc.scalar.activation(out=hb[:, 4:8, :], in_=hf[p_][:, 4:8, :], func=AF.Identity))

            r_free = None
            out_tok = None
            for T in range(NT):
                p = T % 2
                tsl = slice(T * TT, (T + 1) * TT)
                if T + 1 < NT:
                    load(T + 1)
                if T == 0:
                    cast_tok = emit_cast4(0)
                PE.wait(cast_tok, ld_tok[p])
                t_mg_last = None
                for m in range(KC):
                    msl = slice(m * 128, (m + 1) * 128)
                    res = {}
                    for name in ("ga", "gb", "ya", "yb"):
                        bi, bank, bfree = banks.get()
                        PE.wait(bfree)
                        if name in ("ga", "gb"):
                            off = 0 if name == "ga" else 1024
                            for k in range(KC):
                                ins = nc.tensor.matmul(bank[:], lhsT=wg[:, k, off + m * 128:off + (m + 1) * 128], rhs=hb[:, k, :], start=(k == 0), stop=(k == KC - 1))
                        else:
                            wsrc, osrc = (wpa, oa[p]) if name == "ya" else (wpb, obt[p])
                            for k in range(4):
                                ins = nc.tensor.matmul(bank[:], lhsT=wsrc[:, k, msl], rhs=osrc[:, k, :], start=(k == 0), stop=(k == 3))
                        res[name] = (bi, bank, PE.sig(ins))
                    if m == 0:
                        run_deferred()
                    sgs = {}
                    for name in ("ga", "gb"):
                        bi, bank, t_mm = res[name]
                        gi, gbuf, gfree = sg.get()
                        ACT.wait(t_mm, gfree)
                        bcol = m if name == "ga" else 8 + m
                        t_sg = ACT.sig(nc.scalar.activation(out=gbuf[:], in_=bank[:], func=AF.Sigmoid, bias=bg[:, bcol:bcol + 1], scale=1.0))
                        banks.free[bi] = t_sg
                        sgs[name] = (gi, gbuf, t_sg)
                    if m > 0:
                        LN.apply_post(m - 1)
                    (gia, gba, tsa), (gib, gbb, tsb) = sgs["ga"], sgs["gb"]
                    bia, banka, tya = res["ya"]
                    bib, bankb, tyb = res["yb"]
                    DVE.wait(tsa, tya, tsb, tyb)
                    t1 = DVE.sig(nc.vector.tensor_tensor(out=gba[:], in0=gba[:], in1=banka[:], op=ALU.mult))
                    banks.free[bia] = t1
                    t2 = DVE.sig(nc.vector.tensor_tensor(out=gbb[:], in0=gbb[:], in1=bankb[:], op=ALU.mult))
                    banks.free[bib] = t2
                    DVE.wait(t2, mg_free)
                    t_mg = DVE.sig(nc.vector.tensor_tensor(out=mg[:, m, :], in0=gba[:], in1=gbb[:], op=ALU.add))
                    sg.free[gia] = t_mg
                    sg.free[gib] = t_mg
                    t_mg_last = t_mg
                    ta_ = LN.apply_pre(m)
                    if ta_ is not None:
                        r_free = ta_
                LN.apply_post(KC - 1)
                hb_free = res["gb"][2]
                in_free_tok_pe = res["yb"][2]
                if T + 1 < NT:
                    cast_tok = emit_cast4(T + 1)
                PE.wait(t_mg_last)
                DVE.wait(r_free)
                for m in range(KC):
                    msl = slice(m * 128, (m + 1) * 128)
                    bi, bank, bfree = banks.get()
                    PE.wait(bfree)
                    for k in range(KC):
                        ins = nc.tensor.matmul(bank[:], lhsT=wo[:, k, msl], rhs=mg[:, k, :], start=(k == 0), stop=(k == KC - 1))
                    t_mm = PE.sig(ins)
                    DVE.wait(t_mm)
                    t_r = DVE.sig(nc.vector.scalar_tensor_tensor(out=r[:, m, :], in0=hf[p][:, m, :], scalar=ALPHA, in1=bank[:], op0=ALU.mult, op1=ALU.add))
                    banks.free[bi] = t_r
                    stats_chunk(m, t_r, tsl)
                mg_free = t_mm
                in_free[p] = [t_r, in_free_tok_pe]
            run_deferred()
            barrier(LN.flush())

    store_tok_holder = [None]

    def phase_ffa(l, hsrc, after_loads=None):
        with ExitStack() as es:
            A = lambda name, shape, dt: es.enter_context(nc.sbuf_tensor(uniq(name), shape, dt))
            wf = A("wf", [128, KC, 2 * DFF], BF16)
            cwt = A("cwt", [128, 3, FC], F32)
            cbt = A("cbt", [128, FC], F32)
            halo = [A(f"halo{i}", [128, FC, 2], F32) for i in range(2)]
            hf = [A(f"hf{i}", [128, KC, TT], F32) for i in range(2)]
            hb = [A(f"hb{i}", [128, KC, TT], BF16) for i in range(2)]
            gb = Ring([A(f"gb{i}", [128, TT + 2], F32) for i in range(3)])
            t1r = Ring([A(f"t1{i}", [128, TT], F32) for i in range(4)])
            ast = Ring([A(f"as{i}", [128, TT], BF16) for i in range(4)])
            banks = Ring([es.enter_context(nc.psum_tensor(uniq(f"p5b{i}"), [128, 512], F32)) for i in range(7)])
            t_wc = load_w_cast(wf, w_fi[l], KC, 2 * DFF, "w5s", col_split=1408, order=[0, 2, 1, 3])
            if after_loads is not None:
                after_loads()
            so_b = so("p5bias")
            SP.dma(cwt[:], cw_f[l], so_b)
            t_b = SP.dma(cbt[:], cb_f[l], so_b)
            nc.vector.memset(halo[0][:], 0.0)
            t_h0 = DVE.sig(nc.vector.memset(halo[1][:], 0.0))
            DVE.wait(t_b)
            POOL.wait(t_b)
            ACT.wait(t_h0, t_b)
            hsrc_v = hsrc.rearrange("(c p) t -> p c t", p=128)
            actt_v = ACTT.rearrange("(f p) t -> p f t", p=128)
            ld_sem = [so("p5ld0"), so("p5ld1")]
            ld_tok = [None, None]
            hf_free = [None, None]
            hb_free = [None, None]
            cast_tok = [None, None]
            ld_tok[0] = SP.dma(hf[0][:], hsrc_v[:, :, 0:TT], ld_sem[0])
            store_toks = {}

            def emit_cast(T_):
                p_ = T_ % 2
                ACT.wait(ld_tok[p_], hb_free[p_])
                nc.scalar.activation(out=hb[p_][:, 0:4, :], in_=hf[p_][:, 0:4, :], func=AF.Identity)
                tk_ = ACT.sig(nc.scalar.activation(out=hb[p_][:, 4:8, :], in_=hf[p_][:, 4:8, :], func=AF.Identity))
                hf_free[p_] = tk_
                return tk_

            def stage2(item):
                ti, t1b, t_3 = item[0], item[1], item[2]
                ACT.wait(t_3)
                item.append(ACT.sig(nc.scalar.activation(out=t1b[:], in_=t1b[:], func=AF.Silu)))

            def stage3(item):
                ti, t1b, t_3, bi2, bank_u, t_u, f_, tsl_, t_s = item
                ai, abuf, afree = ast.get()
                DVE.wait(t_s, t_u, afree)
                t_a = DVE.sig(nc.vector.tensor_tensor(out=abuf[:], in0=t1b[:], in1=bank_u[:], op=ALU.mult))
                banks.free[bi2] = t_a
                t1r.free[ti] = t_a
                SP.wait(t_a)
                tk = SP.dma(actt_v[:, f_, tsl_], abuf[:], so(f"p5st{ai}"))
                ast.free[ai] = tk
                store_toks[ai] = tk

            pend = []
            for T in range(NT):
                p = T % 2
                tsl = slice(T * TT, (T + 1) * TT)
                hin, hout = halo[T % 2], halo[(T + 1) % 2]
                if T + 1 < NT:
                    SP.wait(hf_free[1 - p])
                    ld_tok[1 - p] = SP.dma(hf[1 - p][:], hsrc_v[:, :, (T + 1) * TT:(T + 2) * TT], ld_sem[1 - p])
                if T == 0:
                    cast_tok[0] = emit_cast(0)
                PE.wait(cast_tok[p])
                for f in range(FC):
                    bi, bank_g, bfree = banks.get()
                    PE.wait(bfree, t_wc[(f * 128) // 1408], t_wc[(DFF + f * 128) // 1408])
                    for k in range(KC):
                        ins = nc.tensor.matmul(bank_g[:], lhsT=wf[:, k, f * 128:(f + 1) * 128], rhs=hb[p][:, k, :], start=(k == 0), stop=(k == KC - 1))
                    t_g = PE.sig(ins)
                    bi2, bank_u, bfree2 = banks.get()
                    PE.wait(bfree2)
                    for k in range(KC):
                        ins = nc.tensor.matmul(bank_u[:], lhsT=wf[:, k, DFF + f * 128:DFF + (f + 1) * 128], rhs=hb[p][:, k, :], start=(k == 0), stop=(k == KC - 1))
                    t_u = PE.sig(ins)
                    if f == FC - 1:
                        hb_free[p] = t_u
                    gi, gbuf, gfree = gb.get()
                    ACT.wait(t_g, gfree)
                    nc.scalar.activation(out=gbuf[:, 0:2], in_=hin[:, f, :], func=AF.Identity)
                    nc.scalar.activation(out=gbuf[:, 2:TT + 2], in_=bank_g[:], func=AF.Identity)
                    t_ge = ACT.sig(nc.scalar.activation(out=hout[:, f, :], in_=bank_g[:, TT - 2:TT], func=AF.Identity))
                    banks.free[bi] = t_ge
                    ti, t1b, tfree = t1r.get()
                    POOL.wait(t_ge, tfree)
                    t_1 = POOL.sig(nc.gpsimd.tensor_scalar(out=t1b[:], in0=gbuf[:, 2:TT + 2], scalar1=cwt[:, 2, f:f + 1], scalar2=cbt[:, f:f + 1],
                                                           op0=ALU.mult, op1=ALU.add))
                    DVE.wait(t_1, t_ge)
                    t_2 = DVE.sig(nc.vector.scalar_tensor_tensor(out=t1b[:], in0=gbuf[:, 1:TT + 1], scalar=cwt[:, 1, f:f + 1], in1=t1b[:], op0=ALU.mult, op1=ALU.add))
                    DVE.wait(t_2)
                    t_3 = DVE.sig(nc.vector.scalar_tensor_tensor(out=t1b[:], in0=gbuf[:, 0:TT], scalar=cwt[:, 0, f:f + 1], in1=t1b[:], op0=ALU.mult, op1=ALU.add))
                    gb.free[gi] = t_3
                    item = [ti, t1b, t_3, bi2, bank_u, t_u, f, tsl]
                    if pend:
                        prev = pend.pop(0)
                        stage2(prev)
                        stage3(prev)
                    pend.append(item)
                    if f == FC // 2 and T + 1 < NT:
                        cast_tok[1 - p] = emit_cast(T + 1)
            while pend:
                prev = pend.pop(0)
                stage2(prev)
                stage3(prev)
            barrier(list(store_toks.values()))

    def phase_ffb(l, hsrc, hdst, pre):
        with ExitStack() as es:
            A = lambda name, shape, dt: es.enter_context(nc.sbuf_tensor(uniq(name), shape, dt))
            wfo, t_w = pre
            g_t = A("g_t", [128, KC], F32)
            b_t = A("b_t", [128, KC], F32)
            hf = [A(f"hf{i}", [128, KC, TT], F32) for i in range(2)]
            at = [A(f"at{i}", [128, FC, TT], BF16) for i in range(2)]
            r = A("r", [128, KC, TT], F32)
            sq = Ring([A(f"sq{i}", [128, TT], F32) for i in range(4)])
            outf = A("outf", [128, KC, TT], F32)
            lnb = [A(f"ln{i}", [128, TT], F32) for i in range(5)]
            tmp2 = [A(f"lt{i}", [128, TT], F32) for i in range(2)]
            banks = Ring([es.enter_context(nc.psum_tensor(uniq(f"p6b{i}"), [128, 512], F32)) for i in range(5)])
            ps_sum = es.enter_context(nc.psum_tensor(uniq("p6s"), [128, 512], F32))
            ps_sq = es.enter_context(nc.psum_tensor(uniq("p6q"), [128, 512], F32))
            so_b = so("p6bias")
            SP.dma(g_t[:], ln2g_f[l], so_b)
            t_b = SP.dma(b_t[:], ln2b_f[l], so_b)
            ACT.wait(t_b)
            PE.wait(t_w)
            hsrc_v = hsrc.rearrange("(c p) t -> p c t", p=128)
            hdst_v = hdst.rearrange("(c p) t -> p c t", p=128)
            actt_v = ACTT.rearrange("(f p) t -> p f t", p=128)
            ld_sem = [so("p6ld0"), so("p6ld1")]
            ld_tok = [None, None]
            in_free = [None, None]

            def load(T):
                p = T % 2
                SP.wait(in_free[p])
                tsl = slice(T * TT, (T + 1) * TT)
                SP.dma(hf[p][:], hsrc_v[:, :, tsl], ld_sem[p])
                ld_tok[p] = SP.dma(at[p][:], actt_v[:, :, tsl], ld_sem[p])

            load(0)
            LN = LNState((lnb[0], lnb[1], lnb[2], lnb[3], lnb[4], tmp2), r, outf, g_t, b_t, hdst_v, "p6st")
            stats_chunk, run_deferred = make_stats(r, sq, A("acc_r", [128, TT], F32), A("acc_q", [128, TT], F32), ps_sum, ps_sq, LN)
            for T in range(NT):
                p = T % 2
                tsl = slice(T * TT, (T + 1) * TT)
                if T + 1 < NT:
                    load(T + 1)
                PE.wait(ld_tok[p])
                DVE.wait(ld_tok[p])
                for m in range(KC):
                    msl = slice(m * 128, (m + 1) * 128)
                    bi, bank, bfree = banks.get()
                    PE.wait(bfree)
                    for f in range(FC):
                        ins = nc.tensor.matmul(bank[:], lhsT=wfo[:, f, msl], rhs=at[p][:, f, :], start=(f == 0), stop=(f == FC - 1))
                    t_mm = PE.sig(ins)
                    if m == 0:
                        run_deferred()
                    ta_ = LN.apply(m)
                    DVE.wait(t_mm, ta_)
                    t_r = DVE.sig(nc.vector.scalar_tensor_tensor(out=r[:, m, :], in0=hf[p][:, m, :], scalar=ALPHA, in1=bank[:], op0=ALU.mult, op1=ALU.add))
                    banks.free[bi] = t_r
                    stats_chunk(m, t_r, tsl)
                in_free[p] = [t_r, t_mm]
            run_deferred()
            barrier(LN.flush())

    hcur = xT
    for l in range(n_layers):
        last = (l == n_layers - 1)
        ph = 6 * l
        if ph < stop_phase:
            phase1(l, hcur)
        if ph + 1 < stop_phase:
            phase_swa(l)
        with ExitStack() as ws:
            wg = ws.enter_context(nc.sbuf_tensor(uniq("wg"), [128, KC, 2048], BF16))
            wpa = ws.enter_context(nc.sbuf_tensor(uniq("wpa"), [128, 4, D], BF16))
            wpb = ws.enter_context(nc.sbuf_tensor(uniq("wpb"), [128, 4, D], BF16))
            wo = ws.enter_context(nc.sbuf_tensor(uniq("wo"), [128, KC, D], BF16))
            tw = [None]

            def load_post():
                load_w_cast(wg, w_in[l, :, O_GA:O_GA + 2048], KC, 2048, "w4s", col_split=1024)
                load_w_cast(wpa, w_pa[l], 4, D, "w4s", col_split=1024)
                load_w_cast(wpb, w_pb[l], 4, D, "w4s", col_split=1024)
                tw[0] = load_w_cast(wo, w_o[l], KC, D, "w4s", col_split=1024)

            if ph + 2 < stop_phase:
                phase_fox(l, after_loads=load_post)
            if ph + 3 < stop_phase:
                if tw[0] is None:
                    load_post()
                phase_post(l, hcur, H1T, (wg, wpa, wpb, wo, tw[0]))
        with ExitStack() as ws:
            wfo = ws.enter_context(nc.sbuf_tensor(uniq("wfo"), [128, FC, D], BF16))
            tw = [None]

            def load_ffo():
                tw[0] = load_w_cast(wfo, w_fo[l], FC, D, "w6s", col_split=1024)

            if ph + 4 < stop_phase:
                phase_ffa(l, H1T, after_loads=load_ffo)
            if ph + 5 < stop_phase:
                if tw[0] is None:
                    load_ffo()
                phase_ffb(l, H1T, outT if last else H2T, (wfo, tw[0]))
        hcur = H2T
    top.close()
    return nc


def make_consts():
    ident = np.eye(128, dtype=np.float32)
    k = np.arange(128)[:, None]
    q = np.arange(128)[None, :]
    negm = np.where(k <= q, 0.0, -30000.0).astype(np.float32)
    slopes = (2.0 ** (-8.0 * np.arange(1, 9) / 8)).astype(np.float64)
    swam = np.zeros((128, 2, 2, 512), np.float32)
    for g in range(2):
        for hh in range(4):
            s = slopes[g * 4 + hh]
            dist_prev = (q + 128 - k).astype(np.float64)
            mp = np.where(dist_prev < 128, -8.0 * s * dist_prev, -30000.0)
            dist_cur = (q - k).astype(np.float64)
            mc = np.where(dist_cur >= 0, -8.0 * s * dist_cur, -30000.0)
            swam[:, g, 0, hh * 128:(hh + 1) * 128] = mp
            swam[:, g, 1, hh * 128:(hh + 1) * 128] = mc
    return ident, negm, swam


def host_layout(inputs):
    f = lambda a: np.ascontiguousarray(np.asarray(a, dtype=np.float32))
    b_in = f(inputs["b_in"])
    L = b_in.shape[0]
    fm = lambda v: np.ascontiguousarray(v.reshape(L, -1, 128).transpose(0, 2, 1))
    bq = np.concatenate([fm(b_in[:, O_QA:O_QA + 512]), fm(b_in[:, O_KA:O_KA + 128]), fm(b_in[:, O_QB:O_QB + 512]),
                         fm(b_in[:, O_KB:O_KB + 512])], axis=2)
    out = dict(
        b_q=f(bq), b_g=fm(b_in[:, O_GA:O_GA + 2048]), b_f=f(b_in[:, O_F:O_F + 8].reshape(L, 8, 1)),
        b_v=f(np.concatenate([b_in[:, O_VA:O_VA + 128], b_in[:, O_VB:O_VB + 512]], axis=1).reshape(L, 1, 640)),
        ln1g_f=fm(f(inputs["ln_mix_g"])), ln1b_f=fm(f(inputs["ln_mix_b"])),
        ln2g_f=fm(f(inputs["ln_ffn_g"])), ln2b_f=fm(f(inputs["ln_ffn_b"])),
        cw_f=f(f(inputs["conv_w"]).reshape(L, 3, FC, 128).transpose(0, 3, 1, 2)),
        cb_f=fm(f(inputs["conv_b"])),
    )
    return out


_WNAMES = ["w_in", "b_in", "attn_sinks", "w_proj_a", "w_proj_b", "w_out", "ln_mix_g", "ln_mix_b",
           "ln_ffn_g", "ln_ffn_b", "w_ffn_in", "conv_w", "conv_b", "w_ffn_out"]


def kernel(**inputs):
    x = np.asarray(inputs["x"], dtype=np.float32)
    ident, negm, swam = make_consts()
    nc = build_program()
    common = {n: np.ascontiguousarray(np.asarray(inputs[n], dtype=np.float32)) for n in _WNAMES}
    common.update(c_ident=ident, c_negm=negm, c_swam=swam)
    common.update(host_layout(inputs))
    in_maps = []
    for b in range(8):
        m = dict(common)
        m["xT"] = np.ascontiguousarray(x[b].T)
        in_maps.append(m)
    res = run_bass_kernel_spmd(nc, in_maps, core_ids=list(range(8)))
    out = np.stack([np.ascontiguousarray(res.results[b]["outT"].T) for b in range(8)], axis=0)
    return out.astype(np.float32)
```

```python
from contextlib import ExitStack
import numpy as np
import concourse.bass as bass
import concourse.mybir as mybir
from concourse.bass_utils import run_bass_kernel_spmd

F32 = mybir.dt.float32
BF16 = mybir.dt.bfloat16
AF = mybir.ActivationFunctionType
ALU = mybir.AluOpType

D = 1024
S = 8192
DEPTH = 2
TT = 512
NT = S // TT
NB = S // 128
KC = 8
DFF = 2816
FC = DFF // 128
NIN = 4360
ALPHA = float((2 * DEPTH) ** 0.25)
EPS = 1e-5
O_QA, O_KA, O_VA, O_QB, O_KB, O_VB, O_F, O_GA, O_GB = 0, 512, 640, 768, 1280, 1792, 2304, 2312, 3336


class SemObj:
    def __init__(self, nc, name):
        self.cm = nc.semaphore(name)
        self.sem = self.cm.__enter__()
        self.cnt = 0


class Eng:
    def __init__(self, nc, eng, name):
        self.e = eng
        self.s = SemObj(nc, "s_" + name)
        self.waited = {}

    def wait(self, *toks):
        for tok in toks:
            if tok is None:
                continue
            if isinstance(tok, list):
                self.wait(*tok)
                continue
            so, v = tok
            if self.waited.get(id(so), 0) >= v:
                continue
            self.waited[id(so)] = v
            self.e.wait_ge(so.sem, v)

    def sig(self, ins):
        self.s.cnt += 1
        ins.then_inc(self.s.sem, 1)
        return (self.s, self.s.cnt)

    def dma(self, out, in_, so, **kw):
        so.cnt += 16
        self.e.dma_start(out=out, in_=in_, **kw).then_inc(so.sem, 16)
        return (so, so.cnt)


class Ring:
    def __init__(self, bufs):
        self.bufs = bufs
        self.free = [None] * len(bufs)
        self.i = 0

    def get(self):
        idx = self.i % len(self.bufs)
        self.i += 1
        return idx, self.bufs[idx], self.free[idx]


def build_program(dbg=False, n_layers=DEPTH, stop_phase=99):
    nc = bass.Bass("TRN2", target_bir_lowering=False)

    def din(name, shape, dt=F32):
        return nc.dram_tensor(name, list(shape), dt, kind="ExternalInput").ap()

    def dscr(name, shape, dt, out=False):
        return nc.dram_tensor(name, list(shape), dt, kind="ExternalOutput" if (out and dbg) else "Internal").ap()

    xT = din("xT", [D, S])
    w_in = din("w_in", [DEPTH, D, NIN])
    b_in = din("b_in", [DEPTH, NIN])
    sinks = din("attn_sinks", [DEPTH, 8])
    w_pa = din("w_proj_a", [DEPTH, 512, D])
    w_pb = din("w_proj_b", [DEPTH, 512, D])
    w_o = din("w_out", [DEPTH, D, D])
    ln1g = din("ln_mix_g", [DEPTH, D])
    ln1b = din("ln_mix_b", [DEPTH, D])
    ln2g = din("ln_ffn_g", [DEPTH, D])
    ln2b = din("ln_ffn_b", [DEPTH, D])
    w_fi = din("w_ffn_in", [DEPTH, D, 2 * DFF])
    cw = din("conv_w", [DEPTH, 3, DFF])
    cb = din("conv_b", [DEPTH, DFF])
    w_fo = din("w_ffn_out", [DEPTH, DFF, D])
    b_q = din("b_q", [DEPTH, 128, 13])
    b_g = din("b_g", [DEPTH, 128, 16])
    b_f = din("b_f", [DEPTH, 8, 1])
    b_v = din("b_v", [DEPTH, 1, 640])
    ln1g_f = din("ln1g_f", [DEPTH, 128, KC])
    ln1b_f = din("ln1b_f", [DEPTH, 128, KC])
    ln2g_f = din("ln2g_f", [DEPTH, 128, KC])
    ln2b_f = din("ln2b_f", [DEPTH, 128, KC])
    cw_f = din("cw_f", [DEPTH, 128, 3, FC])
    cb_f = din("cb_f", [DEPTH, 128, FC])
    c_ident = din("c_ident", [128, 128])
    c_negm = din("c_negm", [128, 128])
    c_swam = din("c_swam", [128, 2, 2, 512])
    outT = nc.dram_tensor("outT", [D, S], F32, kind="ExternalOutput").ap()

    QAT = dscr("QAT", [512, S], BF16)
    KAT = dscr("KAT", [128, S], BF16)
    VA = dscr("VA", [S, 128], BF16)
    QBT = dscr("QBT", [8, 70, S], BF16)
    KBT = dscr("KBT", [8, 70, S], BF16)
    VB = dscr("VB", [S, 512], BF16)
    OAT = dscr("OAT", [512, S], BF16, out=True)
    OBT = dscr("OBT", [512, S], BF16, out=True)
    H1T = dscr("H1T", [D, S], F32, out=True)
    H2T = dscr("H2T", [D, S], F32, out=True)
    ACTT = dscr("ACTT", [DFF, S], BF16, out=True)
    CDBG = dscr("CDBG", [8, S], F32, out=True)

    PE = Eng(nc, nc.tensor, "pe")
    ACT = Eng(nc, nc.scalar, "act")
    DVE = Eng(nc, nc.vector, "dve")
    POOL = Eng(nc, nc.gpsimd, "pool")
    SP = Eng(nc, nc.sync, "sp")
    ENGS = [PE, ACT, DVE, POOL, SP]
    sems = {}
    ucnt = [0]

    def uniq(name):
        ucnt[0] += 1
        return f"{name}_{ucnt[0]}"

    def so(name):
        if name not in sems:
            sems[name] = SemObj(nc, name)
        return sems[name]

    top = ExitStack()
    identf = top.enter_context(nc.sbuf_tensor("identf", [128, 128], F32))
    identb = top.enter_context(nc.sbuf_tensor("identb", [128, 128], BF16))
    negmb = top.enter_context(nc.sbuf_tensor("negmb", [128, 128], BF16))
    onesf = top.enter_context(nc.sbuf_tensor("onesf", [128, 128], F32))
    tiny = top.enter_context(nc.sbuf_tensor("tiny", [128, 8], F32))

    t0 = SP.dma(identf[:], c_ident, so("c0"))
    t1 = POOL.dma(identb[:], c_ident, so("c1"))
    t2 = POOL.dma(negmb[:], c_negm, so("c2"))
    DVE.wait(t0, t1, t2)
    nc.vector.memset(onesf[:], 1.0)
    t_const = DVE.sig(nc.vector.memset(tiny[:], 0.0))
    for E in ENGS:
        E.wait(t_const)

    def barrier(extra=()):
        toks = list(extra)
        toks.append(ACT.sig(nc.scalar.activation(out=tiny[0:1, 0:1], in_=tiny[0:1, 4:5], func=AF.Identity)))
        toks.append(DVE.sig(nc.vector.memset(tiny[0:1, 1:2], 0.0)))
        toks.append(POOL.sig(nc.gpsimd.memset(tiny[0:1, 2:3], 0.0)))
        for E in ENGS:
            E.wait(*toks)

    def load_w_cast(dst3, src2, nchunk, ncols, semname, col_split=2048, order=None):
        toks = []
        srcv = src2.rearrange("(c p) n -> p c n", p=128)
        starts = list(range(0, ncols, col_split))
        if order is None:
            for c0 in starts:
                c1 = min(ncols, c0 + col_split)
                toks.append(POOL.dma(dst3[:, :, c0:c1], srcv[:, :, c0:c1], so(semname)))
            return toks[-1]
        out = {}
        for ci in order:
            c0 = starts[ci]
            c1 = min(ncols, c0 + col_split)
            out[ci] = POOL.dma(dst3[:, :, c0:c1], srcv[:, :, c0:c1], so(f"{semname}_{ci}"))
        return out

    class LNState:
        def __init__(self, lnbufs, r, outf, g_t, b_t, hdst_v, sempfx):
            self.mean_s, self.msq, self.var, self.rstd, self.nmr, self.tmp = lnbufs
            self.r, self.outf, self.g_t, self.b_t, self.hdst_v, self.sempfx = r, outf, g_t, b_t, hdst_v, sempfx
            self.pending = None
            self.tmp_free = [None, None]
            self.st_tok = [None] * KC
            self.pre_tok = [None] * KC
            self.last_dve = None
            self.last_pool = None
            self.stats_free = None
            self.t_nmr = None

        def finalize(self, tsl, ps_sum, ps_sq, t_stats):
            assert self.pending is None
            ACT.wait(t_stats)
            t_mean = ACT.sig(nc.scalar.activation(out=self.mean_s[:], in_=ps_sum[:], func=AF.Identity, scale=1.0 / D))
            DVE.wait(t_mean, t_stats)
            t_msq = DVE.sig(nc.vector.tensor_tensor(out=self.msq[:], in0=self.mean_s[:], in1=self.mean_s[:], op=ALU.mult))
            DVE.wait(t_msq)
            t_var = DVE.sig(nc.vector.scalar_tensor_tensor(out=self.var[:], in0=ps_sq[:], scalar=1.0 / D, in1=self.msq[:], op0=ALU.mult, op1=ALU.subtract))
            ACT.wait(t_var)
            t_ln = ACT.sig(nc.scalar.activation(out=self.var[:], in_=self.var[:], func=AF.Ln, bias=eps_t[:, 0:1], scale=1.0))
            ACT.wait(t_ln, self.last_dve)
            t_rstd = ACT.sig(nc.scalar.activation(out=self.rstd[:], in_=self.var[:], func=AF.Exp, scale=-0.5))
            self.stats_free = t_rstd
            DVE.wait(t_rstd, self.last_pool)
            self.t_nmr = DVE.sig(nc.vector.scalar_tensor_tensor(out=self.nmr[:], in0=self.mean_s[:], scalar=-1.0, in1=self.rstd[:], op0=ALU.mult, op1=ALU.mult))
            self.pending = tsl

        def apply_pre(self, m):
            if self.pending is None:
                return None
            tb = self.tmp[m % 2]
            DVE.wait(self.tmp_free[m % 2], self.t_nmr)
            t_a = DVE.sig(nc.vector.tensor_tensor(out=tb[:], in0=self.r[:, m, :], in1=self.rstd[:], op=ALU.mult))
            POOL.wait(t_a, self.t_nmr)
            t_t = POOL.sig(nc.gpsimd.tensor_tensor(out=tb[:], in0=tb[:], in1=self.nmr[:], op=ALU.add))
            self.pre_tok[m] = t_t
            self.last_dve, self.last_pool = t_a, t_t
            return t_a

        def apply_post(self, m):
            if self.pending is None or self.pre_tok[m] is None:
                return
            tsl = self.pending
            tb = self.tmp[m % 2]
            ACT.wait(self.pre_tok[m], self.st_tok[m])
            t_o = ACT.sig(nc.scalar.activation(out=self.outf[:, m, :], in_=tb[:], func=AF.Identity,
                                               scale=self.g_t[:, m:m + 1], bias=self.b_t[:, m:m + 1]))
            self.tmp_free[m % 2] = t_o
            self.pre_tok[m] = None
            SP.wait(t_o)
            self.st_tok[m] = SP.dma(self.hdst_v[:, m, tsl], self.outf[:, m, :], so(f"{self.sempfx}{m}"))
            if m == KC - 1:
                self.pending = None

        def apply(self, m):
            t_a = self.apply_pre(m)
            self.apply_post(m)
            return t_a

        def flush(self):
            for m in range(KC):
                self.apply(m)
            return [t for t in self.st_tok if t is not None]

    def make_stats(r, sq, acc_r, acc_q, ps_sum, ps_sq, LN):
        st = {"ar": None, "aq": None, "pe": None, "deferred": None}

        def stats_chunk(m, t_r, tsl):
            qi, sqb, qfree = sq.get()
            ACT.wait(t_r, qfree)
            t_q = ACT.sig(nc.scalar.activation(out=sqb[:], in_=r[:, m, :], func=AF.Square))
            if m == 0:
                DVE.wait(t_r, st["pe"])
                st["ar"] = DVE.sig(nc.vector.tensor_copy(out=acc_r[:], in_=r[:, 0, :]))
                POOL.wait(t_q, st["pe"])
                st["aq"] = POOL.sig(nc.gpsimd.tensor_copy(out=acc_q[:], in_=sqb[:]))
            else:
                DVE.wait(t_r, st["ar"])
                st["ar"] = DVE.sig(nc.vector.tensor_tensor(out=acc_r[:], in0=acc_r[:], in1=r[:, m, :], op=ALU.add))
                if m == KC - 1:
                    DVE.wait(t_q, st["aq"])
                    st["aq"] = DVE.sig(nc.vector.tensor_tensor(out=acc_q[:], in0=acc_q[:], in1=sqb[:], op=ALU.add))
                else:
                    POOL.wait(t_q, st["aq"])
                    st["aq"] = POOL.sig(nc.gpsimd.tensor_tensor(out=acc_q[:], in0=acc_q[:], in1=sqb[:], op=ALU.add))
            sq.free[qi] = st["aq"]
            if m == KC - 1:
                t_ar, t_aq = st["ar"], st["aq"]

                def deferred():
                    PE.wait(t_ar, t_aq, LN.stats_free)
                    nc.tensor.matmul(ps_sum[:], lhsT=onesf[:], rhs=acc_r[:], start=True, stop=True)
                    st["pe"] = PE.sig(nc.tensor.matmul(ps_sq[:], lhsT=onesf[:], rhs=acc_q[:], start=True, stop=True))
                    LN.finalize(tsl, ps_sum, ps_sq, st["pe"])

                st["deferred"] = deferred

        def run_deferred():
            if st["deferred"] is not None:
                st["deferred"]()
                st["deferred"] = None

        return stats_chunk, run_deferred

    eps_t = top.enter_context(nc.sbuf_tensor("eps_t", [128, 1], F32))
    t_eps = DVE.sig(nc.vector.memset(eps_t[:], EPS))
    ACT.wait(t_eps)

    def phase1(l, hsrc):
        with ExitStack() as es:
            A = lambda name, shape, dt: es.enter_context(nc.sbuf_tensor(uniq(name), shape, dt))
            w1 = A("w1", [128, KC, 2312], BF16)
            bq = A("bq", [128, 13], F32)
            bfn = A("bfn", [8, 1], F32)
            bv = A("bv", [128, 640], F32)
            hf = [A(f"hf{i}", [128, KC, TT], F32) for i in range(2)]
            hb = [A(f"hb{i}", [128, KC, TT], BF16) for i in range(2)]
            st = Ring([A(f"st{i}", [128, TT], BF16) for i in range(4)])
            sv = Ring([A(f"sv{i}", [128, 4, 640], BF16) for i in range(2)])
            cc = A("cc", [8, S], F32)
            et = A("et", [8, TT], F32)
            lt = A("lt", [8, TT], F32)
            shb = Ring([(A(f"kp{i}", [8, 3, TT], BF16), A(f"qp{i}", [8, 3, TT], BF16), A(f"r1_{i}", [8, TT], F32), A(f"r2_{i}", [8, TT], F32)) for i in range(2)])
            ones8 = A("ones8", [8, 3, TT], BF16)
            eights = A("eights", [8, 3, TT], BF16)
            banks = Ring([es.enter_context(nc.psum_tensor(uniq(f"p1b{i}"), [128, 512], F32)) for i in range(6)])

            t_w = load_w_cast(w1, w_in[l, :, 0:2312], KC, 2312, "w1s", col_split=1156)
            so_b = so("p1bias")
            SP.dma(bq[:], b_q[l], so_b)
            SP.dma(bfn[:], b_f[l], so_b)
            t_b = SP.dma(bv[:], b_v[l].partition_broadcast(128), so_b)
            DVE.wait(t_b)
            nc.vector.tensor_scalar(out=bfn[:], in0=bfn[:], scalar1=-1.0, scalar2=None, op0=ALU.mult)
            nc.vector.memset(eights[:], 8.0)
            t_init = DVE.sig(nc.vector.memset(ones8[:], 1.0))
            ACT.wait(t_b, t_init)
            SP.wait(t_init)
            PE.wait(t_w)

            hsrc_v = hsrc.rearrange("(c p) t -> p c t", p=128)
            ld_sem = [so("p1ld0"), so("p1ld1")]
            hf_free = [None, None]
            hb_free = [None, None]
            ld_tok = [None, None]
            ld_tok[0] = SP.dma(hf[0][:], hsrc_v[:, :, 0:TT], ld_sem[0])
            store_toks = {}
            out_specs = []
            for c in range(4):
                out_specs.append((O_QA + c * 128, ("QA", c)))
            out_specs.append((O_KA, ("KA", 0)))
            for c in range(4):
                out_specs.append((O_QB + c * 128, ("QB", c)))
            for c in range(4):
                out_specs.append((O_KB + c * 128, ("KB", c)))
            prev_c_last = None
            cast_tok = [None, None]

            def emit_cast(T_):
                p_ = T_ % 2
                ACT.wait(ld_tok[p_], hb_free[p_])
                nc.scalar.activation(out=hb[p_][:, 0:4, :], in_=hf[p_][:, 0:4, :], func=AF.Identity)
                tk_ = ACT.sig(nc.scalar.activation(out=hb[p_][:, 4:8, :], in_=hf[p_][:, 4:8, :], func=AF.Identity))
                hf_free[p_] = tk_
                return tk_

            for T in range(NT):
                p = T % 2
                tsl = slice(T * TT, (T + 1) * TT)
                if T + 1 < NT:
                    SP.wait(hf_free[1 - p])
                    ld_tok[1 - p] = SP.dma(hf[1 - p][:], hsrc_v[:, :, (T + 1) * TT:(T + 2) * TT], ld_sem[1 - p])
                if T == 0:
                    cast_tok[0] = emit_cast(0)
                PE.wait(cast_tok[p])
                bi, bank, bfree = banks.get()
                PE.wait(bfree)
                for k in range(KC):
                    ins = nc.tensor.matmul(bank[0:8, :], lhsT=w1[:, k, O_F:O_F + 8], rhs=hb[p][:, k, :], start=(k == 0), stop=(k == KC - 1))
                t_mm = PE.sig(ins)
                ACT.wait(t_mm, prev_c_last)
                t_e = ACT.sig(nc.scalar.activation(out=et[:], in_=bank[0:8, :], func=AF.Exp, scale=-1.0, bias=bfn[:, 0:1]))
                ACT.wait(t_e)
                t_lf = ACT.sig(nc.scalar.activation(out=lt[:], in_=et[:], func=AF.Ln, bias=1.0, scale=1.0))
                banks.free[bi] = t_lf
                DVE.wait(t_lf)
                init = 0.0 if T == 0 else cc[:, T * TT - 1:T * TT]
                t_sc = DVE.sig(nc.vector.tensor_tensor_scan(out=cc[:, tsl], data0=lt[:], data1=lt[:], initial=init, op0=ALU.add, op1=ALU.bypass))
                hi, (kp, qp, r1, r2), shfree = shb.get()
                DVE.wait(shfree, t_sc)
                t_a = DVE.sig(nc.vector.tensor_copy(out=kp[:, 0, :], in_=cc[:, tsl]))
                DVE.wait(t_a)
                t_b_ = DVE.sig(nc.vector.tensor_tensor(out=r1[:], in0=cc[:, tsl], in1=kp[:, 0, :], op=ALU.subtract))
                DVE.wait(t_b_)
                t_c_ = DVE.sig(nc.vector.tensor_copy(out=kp[:, 1, :], in_=r1[:]))
                DVE.wait(t_c_)
                t_d_ = DVE.sig(nc.vector.tensor_tensor(out=r2[:], in0=r1[:], in1=kp[:, 1, :], op=ALU.subtract))
                DVE.wait(t_d_)
                t_e_ = DVE.sig(nc.vector.tensor_copy(out=kp[:, 2, :], in_=r2[:]))
                DVE.wait(t_e_)
                t_sh = DVE.sig(nc.vector.tensor_scalar(out=qp[:], in0=kp[:], scalar1=-8.0, scalar2=None, op0=ALU.mult))
                prev_c_last = t_sh
                SP.wait(t_sh)
                hsem = so(f"p1sh{hi}")
                SP.dma(KBT[:, 67:70, tsl], kp[:], hsem)
                SP.dma(QBT[:, 64:67, tsl], qp[:], hsem)
                SP.dma(KBT[:, 64:67, tsl], ones8[:], hsem)
                tk = SP.dma(QBT[:, 67:70, tsl], eights[:], hsem)
                shb.free[hi] = tk
                store_toks[("sh", hi)] = tk
                for oi, (wo, (kind, c)) in enumerate(out_specs):
                    if oi == 6 and T + 1 < NT:
                        cast_tok[1 - p] = emit_cast(T + 1)
                    bi, bank, bfree = banks.get()
                    PE.wait(bfree)
                    for k in range(KC):
                        ins = nc.tensor.matmul(bank[:], lhsT=w1[:, k, wo:wo + 128], rhs=hb[p][:, k, :], start=(k == 0), stop=(k == KC - 1))
                    t_mm = PE.sig(ins)
                    si, sbuf, sfree = st.get()
                    ACT.wait(t_mm, sfree)
                    t_ev = ACT.sig(nc.scalar.activation(out=sbuf[:], in_=bank[:], func=AF.Identity, bias=bq[:, oi:oi + 1], scale=1.0))
                    banks.free[bi] = t_ev
                    SP.wait(t_ev)
                    ssem = so(f"p1st{si}")
                    if kind == "QA":
                        tk = SP.dma(QAT[c * 128:(c + 1) * 128, tsl], sbuf[:], ssem)
                    elif kind == "KA":
                        tk = SP.dma(KAT[:, tsl], sbuf[:], ssem)
                    else:
                        dst = QBT if kind == "QB" else KBT
                        SP.dma(dst[2 * c, 0:64, tsl], sbuf[0:64, :], ssem)
                        tk = SP.dma(dst[2 * c + 1, 0:64, tsl], sbuf[64:128, :], ssem)
                    st.free[si] = tk
                    store_toks[("st", si)] = tk
                vi, svbuf, svfree = sv.get()
                DVE.wait(svfree)
                for b in range(4):
                    bi, bank, bfree = banks.get()
                    PE.wait(bfree)
                    for k in range(KC):
                        ins = nc.tensor.matmul(bank[:], lhsT=hb[p][:, k, b * 128:(b + 1) * 128], rhs=w1[:, k, O_VB:O_VB + 512], start=(k == 0), stop=(k == KC - 1))
                    t_mb = PE.sig(ins)
                    bi2, bank2, bfree2 = banks.get()
                    PE.wait(bfree2)
                    for k in range(KC):
                        ins = nc.tensor.matmul(bank2[:, 0:128], lhsT=hb[p][:, k, b * 128:(b + 1) * 128], rhs=w1[:, k, O_VA:O_VA + 128], start=(k == 0), stop=(k == KC - 1))
                    t_ma = PE.sig(ins)
                    DVE.wait(t_mb, t_ma)
                    t_e1 = DVE.sig(nc.vector.tensor_tensor(out=svbuf[:, b, 128:640], in0=bank[:], in1=bv[:, 128:640], op=ALU.add))
                    t_e2 = DVE.sig(nc.vector.tensor_tensor(out=svbuf[:, b, 0:128], in0=bank2[:, 0:128], in1=bv[:, 0:128], op=ALU.add))
                    banks.free[bi] = t_e1
                    banks.free[bi2] = t_e2
                hb_free[p] = t_ma
                SP.wait(t_e2)
                vsem = so(f"p1sv{vi}")
                SP.dma(VA[tsl, :].rearrange("(b p) d -> p b d", p=128), svbuf[:, :, 0:128], vsem)
                tk = SP.dma(VB[tsl, :].rearrange("(b p) d -> p b d", p=128), svbuf[:, :, 128:640], vsem)
                sv.free[vi] = tk
                store_toks[("sv", vi)] = tk
            if dbg:
                SP.wait(prev_c_last)
                store_toks["cdbg"] = SP.dma(CDBG, cc[:], so("cdbg"))
            barrier([v for v in store_toks.values()])

    def phase_swa(l):
        with ExitStack() as es:
            A = lambda name, shape, dt: es.enter_context(nc.sbuf_tensor(uniq(name), shape, dt))
            qa = A("qa", [64, 4, S], BF16)
            ka = A("ka", [64, S], BF16)
            va = A("va", [128, NB, 128], BF16)
            mk = A("mk", [128, 2, 2, 512], BF16)
            esk = A("esk", [128, 8], F32)
            zf = A("zf", [128, 128], F32)
            esk512 = A("esk512", [128, 2, 512], F32)
            pb = Ring([A(f"pb{i}", [128, 512], BF16) for i in range(4)])
            dnr = Ring([A(f"dn{i}", [128, 512], F32) for i in range(2)])
            lnr = Ring([A(f"lnr{i}", [128, 512], F32) for i in range(2)])
            rcr = Ring([A(f"rcr{i}", [128, 512], F32) for i in range(2)])
            ost = Ring([A(f"ost{i}", [64, 4, TT], BF16) for i in range(2)])
            sb = Ring([es.enter_context(nc.psum_tensor(uniq(f"swS{i}"), [128, 512], F32)) for i in range(4)])
            ob = Ring([es.enter_context(nc.psum_tensor(uniq(f"swO{i}"), [128, 512], F32)) for i in range(3)])
            so_c = so("swc")
            t_mk = POOL.dma(mk[:], c_swam, so("swmk"))
            t_c = SP.dma(esk[:], sinks[l].rearrange("(o n) -> o n", o=1).partition_broadcast(128), so_c)
            ACT.wait(t_c)
            t_es = ACT.sig(nc.scalar.activation(out=esk[:], in_=esk[:], func=AF.Exp))
            t_ones = POOL.sig(nc.gpsimd.memset(va[:, :, 64:128], 1.0))
            t_z = DVE.sig(nc.vector.memset(zf[:], 0.0))
            DVE.wait(t_es, t_z)
            for g in range(2):
                for hh in range(4):
                    t_e5 = DVE.sig(nc.vector.tensor_scalar(out=esk512[64:128, g, hh * 128:(hh + 1) * 128], in0=zf[64:128, :],
                                                           scalar1=esk[64:128, g * 4 + hh:g * 4 + hh + 1], scalar2=None, op0=ALU.add))
            DVE.wait(t_e5)
            PE.wait(t_mk)
            store_toks = {}
            last_use = None
            chunk_free = [None] * 4
            pending_norm = [None]
            NCH = 4
            CW = S // NCH
            ch_tok = [[None] * NCH, [None] * NCH]

            def emit_load(g_, c):
                SP.wait(chunk_free[c])
                so_l = so(f"swld{c}")
                csl = slice(c * CW, (c + 1) * CW)
                SP.dma(qa[:, 0:2, csl], QAT[g_ * 256:g_ * 256 + 128, csl].rearrange("(h d) t -> d h t", d=64), so_l)
                SP.dma(qa[:, 2:4, csl], QAT[g_ * 256 + 128:g_ * 256 + 256, csl].rearrange("(h d) t -> d h t", d=64), so_l)
                SP.dma(ka[:, csl], KAT[g_ * 64:(g_ + 1) * 64, csl], so_l)
                return SP.dma(va[:, c * (NB // NCH):(c + 1) * (NB // NCH), 0:64],
                              VA[csl, g_ * 64:(g_ + 1) * 64].rearrange("(n p) d -> p n d", p=128), so_l)

            for g in range(2):
                if g == 0:
                    SP.wait(t_ones)
                    for c in range(NCH):
                        ch_tok[0][c] = emit_load(0, c)
                oi = osb = None
                def stage_a(n, g=g):
                    if n % (NB // NCH) == 0:
                        PE.wait(ch_tok[g][n // (NB // NCH)])
                    qsl = slice(n * 128, (n + 1) * 128)
                    parts = ([("prev", n - 1)] if n > 0 else []) + [("cur", n)]
                    pbs = []
                    for (which, kb) in parts:
                        midx = 0 if which == "prev" else 1
                        si, sbank, sfree = sb.get()
                        PE.wait(sfree)
                        nc.tensor.matmul(sbank[:].rearrange("p (h q) -> p h q", h=4), lhsT=ka[:, kb * 128:(kb + 1) * 128], rhs=qa[:, :, qsl], start=True, stop=False)
                        t_s = PE.sig(nc.tensor.matmul(sbank[:], lhsT=identb[:], rhs=mk[:, g, midx, :], start=False, stop=True))
                        bi_, pbb, pbfree = pb.get()
                        ACT.wait(t_s, pbfree)
                        t_p = ACT.sig(nc.scalar.activation(out=pbb[:], in_=sbank[:], func=AF.Exp, scale=0.125))
                        sb.free[si] = t_p
                        pbs.append((bi_, pbb, t_p, kb))
                    return pbs

                pbs_next = stage_a(0)
                for n in range(NB):
                    pbs = pbs_next
                    if n + 1 < NB:
                        pbs_next = stage_a(n + 1)
                    if n % 4 == 0:
                        oi, osb, ofree = ost.get()
                        DVE.wait(ofree)
                    obi, obank, obfree = ob.get()
                    PE.wait(obfree)
                    for ii, (bi_, pbb, t_p, kb) in enumerate(pbs):
                        PE.wait(t_p)
                        t_o = PE.sig(nc.tensor.matmul(obank[:], lhsT=va[:, kb, :], rhs=pbb[:], start=(ii == 0), stop=(ii == len(pbs) - 1)))
                        pb.free[bi_] = t_o
                    di, dnb, dfree = dnr.get()
                    DVE.wait(t_o, dfree)
                    t_d = DVE.sig(nc.vector.tensor_tensor(out=dnb[64:128, :], in0=obank[64:128, :], in1=esk512[64:128, g, :], op=ALU.add))

                    def norm(n=n, g=g, obi=obi, obank=obank, di=di, dnb=dnb, t_d=t_d, oi=oi, osb=osb):
                        li, lnb_, lfree = lnr.get()
                        ACT.wait(t_d, lfree)
                        t_l = ACT.sig(nc.scalar.activation(out=lnb_[64:128, :], in_=dnb[64:128, :], func=AF.Ln))
                        dnr.free[di] = t_l
                        ri, rcb, rfree = rcr.get()
                        ACT.wait(t_l, rfree)
                        t_r = ACT.sig(nc.scalar.activation(out=rcb[64:128, :], in_=lnb_[64:128, :], func=AF.Exp, scale=-1.0))
                        lnr.free[li] = t_r
                        DVE.wait(t_r)
                        t_n = DVE.sig(nc.vector.tensor_tensor(out=osb[:, :, (n % 4) * 128:(n % 4 + 1) * 128],
                                                              in0=obank[0:64, :].rearrange("p (h q) -> p h q", h=4),
                                                              in1=rcb[64:128, :].rearrange("p (h q) -> p h q", h=4), op=ALU.mult))
                        ob.free[obi] = t_n
                        rcr.free[ri] = t_n
                        if n % 4 == 3:
                            SP.wait(t_n)
                            T = n // 4
                            osem = so(f"swst{oi}")
                            SP.dma(OAT[g * 256:g * 256 + 128, T * TT:(T + 1) * TT].rearrange("(h d) t -> d h t", d=64), osb[:, 0:2, :], osem)
                            tk = SP.dma(OAT[g * 256 + 128:g * 256 + 256, T * TT:(T + 1) * TT].rearrange("(h d) t -> d h t", d=64), osb[:, 2:4, :], osem)
                            ost.free[oi] = tk
                            store_toks[oi] = tk

                    if pending_norm[0] is not None:
                        pending_norm[0]()
                    pending_norm[0] = norm
                    last_use = t_o
                    if n % (NB // NCH) == 0 and n > 0:
                        chunk_free[n // (NB // NCH) - 1] = t_o
                        if g == 0:
                            ch_tok[1][n // (NB // NCH) - 1] = emit_load(1, n // (NB // NCH) - 1)
                    if n == NB - 1:
                        chunk_free[NCH - 1] = t_o
                        if g == 0:
                            ch_tok[1][NCH - 1] = emit_load(1, NCH - 1)
                if pending_norm[0] is not None:
                    pending_norm[0]()
                    pending_norm[0] = None
            barrier(list(store_toks.values()))

    def phase_fox(l, after_loads=None):
        with ExitStack() as es:
            A = lambda name, shape, dt: es.enter_context(nc.sbuf_tensor(uniq(name), shape, dt))
            qb = [A(f"fq{i}", [70, S], BF16) for i in range(2)]
            kb = [A(f"fk{i}", [70, S], BF16) for i in range(2)]
            vb = [A(f"fv{i}", [128, NB, 128], BF16) for i in range(2)]
            pr = Ring([A(f"fp{i}", [128, 2 * TT], BF16) for i in range(3)])
            rc = Ring([A(f"frc{i}", [128, 512], F32) for i in range(2)])
            ost = Ring([A(f"fo{i}", [64, 512], BF16) for i in range(2)])
            sall = es.enter_context(nc.psum_tensor(uniq("fS"), [128, 6 * TT], F32))
            sb = Ring([sall[:, i * 2 * TT:(i + 1) * 2 * TT] for i in range(3)])
            ob = Ring([es.enter_context(nc.psum_tensor(uniq(f"fO{i}"), [128, 512], F32)) for i in range(2)])
            nc.gpsimd.memset(vb[0][:, :, 64:128], 1.0)
            t_ones = POOL.sig(nc.gpsimd.memset(vb[1][:, :, 64:128], 1.0))
            SP.wait(t_ones)
            ld_tok = [None, None]
            buf_free = [None, None]
            store_toks = {}

            def load_head(h):
                p = h % 2
                SP.wait(buf_free[p])
                sl = so(f"fld{p}")
                SP.dma(qb[p][:], QBT[h], sl)
                SP.dma(kb[p][:], KBT[h], sl)
                ld_tok[p] = SP.dma(vb[p][:, :, 0:64], VB[:, h * 64:(h + 1) * 64].rearrange("(n p) d -> p n d", p=128), sl)

            load_head(0)
            if after_loads is not None:
                after_loads()
            for h in range(8):
                p = h % 2
                if h + 1 < 8:
                    load_head(h + 1)
                PE.wait(ld_tok[p])
                Q, K, V = qb[p], kb[p], vb[p]
                units = []
                for T in range(NT):
                    nj = 4 * T + 4
                    for j in range(0, 4 * T, 2):
                        units.append((T, j, nj, True))
                    for j in range(4 * T, nj):
                        units.append((T, j, nj, False))
                state = {}

                def emit_S(u):
                    T, j, nj, pair = u
                    si, sbank, sfree = sb.get()
                    PE.wait(sfree)
                    q0 = T * TT
                    if pair:
                        nc.tensor.matmul(sbank[:, 0:TT], lhsT=K[:, j * 128:(j + 1) * 128], rhs=Q[:, q0:q0 + TT], start=True, stop=True)
                        t_s = PE.sig(nc.tensor.matmul(sbank[:, TT:2 * TT], lhsT=K[:, (j + 1) * 128:(j + 2) * 128], rhs=Q[:, q0:q0 + TT], start=True, stop=True))
                        cols = (0, 2 * TT)
                    else:
                        c0 = (j - 4 * T) * 128
                        nc.tensor.matmul(sbank[:, c0:c0 + 128], lhsT=K[:, j * 128:(j + 1) * 128], rhs=Q[:, q0 + c0:q0 + c0 + 128], start=True, stop=False)
                        t_s = PE.sig(nc.tensor.matmul(sbank[:, c0:c0 + 128], lhsT=identb[:], rhs=negmb[:], start=False, stop=True))
                        if c0 + 128 < TT:
                            t_s = PE.sig(nc.tensor.matmul(sbank[:, c0 + 128:TT], lhsT=K[:, j * 128:(j + 1) * 128], rhs=Q[:, q0 + c0 + 128:q0 + TT], start=True, stop=True))
                        cols = (c0, TT)
                    pi, pbuf, pfree = pr.get()
                    ACT.wait(t_s, pfree)
                    a_, b_ = cols
                    t_p = ACT.sig(nc.scalar.activation(out=pbuf[:, a_:b_], in_=sbank[:, a_:b_], func=AF.Exp, scale=0.125))
                    sb.free[si] = t_p
                    return (pi, pbuf, t_p, cols)

                def emit_PV(u, sres):
                    T, j, nj, pair = u
                    pi, pbuf, t_p, (a_, b_) = sres
                    if j == 0:
                        obi, obank, obfree = ob.get()
                        PE.wait(obfree)
                        state[("o", T)] = (obi, obank)
                    obi, obank = state[("o", T)]
                    PE.wait(t_p)
                    if pair:
                        nc.tensor.matmul(obank[:], lhsT=V[:, j, :], rhs=pbuf[:, 0:TT], start=(j == 0), stop=False)
                        t_o = PE.sig(nc.tensor.matmul(obank[:], lhsT=V[:, j + 1, :], rhs=pbuf[:, TT:2 * TT], start=False, stop=False))
                    else:
                        t_o = PE.sig(nc.tensor.matmul(obank[:, a_:b_], lhsT=V[:, j, :], rhs=pbuf[:, a_:b_], start=(j == 0), stop=(j == nj - 1)))
                    pr.free[pi] = t_o
                    if j == nj - 1:
                        ri, rcb, rfree = rc.get()
                        DVE.wait(t_o, rfree)
                        t_rc = DVE.sig(nc.vector.reciprocal(out=rcb[64:128, :], in_=obank[64:128, :]))
                        oi, osb, ofree = ost.get()
                        DVE.wait(ofree, t_rc)
                        t_n = DVE.sig(nc.vector.tensor_tensor(out=osb[:], in0=obank[0:64, :], in1=rcb[64:128, :], op=ALU.mult))
                        ob.free[obi] = t_n
                        rc.free[ri] = t_n
                        SP.wait(t_n)
                        tk = SP.dma(OBT[h * 64:(h + 1) * 64, T * TT:(T + 1) * TT], osb[:], so(f"fst{oi}"))
                        ost.free[oi] = tk
                        store_toks[oi] = tk
                    return t_o

                LOOK = 2
                sres = {}
                t_o = None
                for i in range(len(units) + LOOK):
                    if i < len(units):
                        sres[i] = emit_S(units[i])
                    if i - LOOK >= 0:
                        t_o = emit_PV(units[i - LOOK], sres.pop(i - LOOK))
                buf_free[p] = t_o
            barrier(list(store_toks.values()))

    def phase_post(l, hsrc, hdst, pre):
        with ExitStack() as es:
            A = lambda name, shape, dt: es.enter_context(nc.sbuf_tensor(uniq(name), shape, dt))
            wg, wpa, wpb, wo, t_w = pre
            bg = A("bg", [128, 16], F32)
            g_t = A("g_t", [128, KC], F32)
            b_t = A("b_t", [128, KC], F32)
            hf = [A(f"hf{i}", [128, KC, TT], F32) for i in range(2)]
            hb = A("hb", [128, KC, TT], BF16)
            oa = [A(f"oa{i}", [128, 4, TT], BF16) for i in range(2)]
            obt = [A(f"ob{i}", [128, 4, TT], BF16) for i in range(2)]
            sg = Ring([A(f"sg{i}", [128, TT], F32) for i in range(4)])
            mg = A("mg", [128, KC, TT], BF16)
            r = A("r", [128, KC, TT], F32)
            sq = Ring([A(f"sq{i}", [128, TT], F32) for i in range(4)])
            outf = A("outf", [128, KC, TT], F32)
            lnb = [A(f"ln{i}", [128, TT], F32) for i in range(5)]
            tmp2 = [A(f"lt{i}", [128, TT], F32) for i in range(2)]
            banks = Ring([es.enter_context(nc.psum_tensor(uniq(f"p4b{i}"), [128, 512], F32)) for i in range(6)])
            ps_sum = es.enter_context(nc.psum_tensor(uniq("p4s"), [128, 512], F32))
            ps_sq = es.enter_context(nc.psum_tensor(uniq("p4q"), [128, 512], F32))

            so_b = so("p4bias")
            SP.dma(bg[:], b_g[l], so_b)
            SP.dma(g_t[:], ln1g_f[l], so_b)
            t_b = SP.dma(b_t[:], ln1b_f[l], so_b)
            ACT.wait(t_b)
            PE.wait(t_w)
            hsrc_v = hsrc.rearrange("(c p) t -> p c t", p=128)
            oat_v = OAT.rearrange("(c p) t -> p c t", p=128)
            obt_v = OBT.rearrange("(c p) t -> p c t", p=128)
            hdst_v = hdst.rearrange("(c p) t -> p c t", p=128)
            ld_sem = [so("p4ld0"), so("p4ld1")]
            ld_tok = [None, None]
            in_free = [None, None]

            def load(T):
                p = T % 2
                SP.wait(in_free[p])
                tsl = slice(T * TT, (T + 1) * TT)
                SP.dma(hf[p][:], hsrc_v[:, :, tsl], ld_sem[p])
                SP.dma(oa[p][:], oat_v[:, :, tsl], ld_sem[p])
                ld_tok[p] = SP.dma(obt[p][:], obt_v[:, :, tsl], ld_sem[p])

            load(0)
            hb_free = None
            mg_free = None
            cast_tok = None
            LN = LNState((lnb[0], lnb[1], lnb[2], lnb[3], lnb[4], tmp2), r, outf, g_t, b_t, hdst_v, "p4st")
            stats_chunk, run_deferred = make_stats(r, sq, A("acc_r", [128, TT], F32), A("acc_q", [128, TT], F32), ps_sum, ps_sq, LN)

            def emit_cast4(T_):
                p_ = T_ % 2
                ACT.wait(ld_tok[p_], hb_free)
                nc.scalar.activation(out=hb[:, 0:4, :], in_=hf[p_][:, 0:4, :], func=AF.Identity)
                return ACT.sig(nc.scalar.activation(out=hb[:, 4:8, :], in_=hf[p_][:, 4:8, :], func=AF.Identity))

            r_free = None
            out_tok = None
            for T in range(NT):
                p = T % 2
                tsl = slice(T * TT, (T + 1) * TT)
                if T + 1 < NT:
                    load(T + 1)
                if T == 0:
                    cast_tok = emit_cast4(0)
                PE.wait(cast_tok, ld_tok[p])
                t_mg_last = None
                for m in range(KC):
                    msl = slice(m * 128, (m + 1) * 128)
                    res = {}
                    for name in ("ga", "gb", "ya", "yb"):
                        bi, bank, bfree = banks.get()
                        PE.wait(bfree)
                        if name in ("ga", "gb"):
                            off = 0 if name == "ga" else 1024
                            for k in range(KC):
                                ins = nc.tensor.matmul(bank[:], lhsT=wg[:, k, off + m * 128:off + (m + 1) * 128], rhs=hb[:, k, :], start=(k == 0), stop=(k == KC - 1))
                        else:
                            wsrc, osrc = (wpa, oa[p]) if name == "ya" else (wpb, obt[p])
                            for k in range(4):
                                ins = nc.tensor.matmul(bank[:], lhsT=wsrc[:, k, msl], rhs=osrc[:, k, :], start=(k == 0), stop=(k == 3))
                        res[name] = (bi, bank, PE.sig(ins))
                    if m == 0:
                        run_deferred()
                    sgs = {}
                    for name in ("ga", "gb"):
                        bi, bank, t_mm = res[name]
                        gi, gbuf, gfree = sg.get()
                        ACT.wait(t_mm, gfree)
                        bcol = m if name == "ga" else 8 + m
                        t_sg = ACT.sig(nc.scalar.activation(out=gbuf[:], in_=bank[:], func=AF.Sigmoid, bias=bg[:, bcol:bcol + 1], scale=1.0))
                        banks.free[bi] = t_sg
                        sgs[name] = (gi, gbuf, t_sg)
                    if m > 0:
                        LN.apply_post(m - 1)
                    (gia, gba, tsa), (gib, gbb, tsb) = sgs["ga"], sgs["gb"]
                    bia, banka, tya = res["ya"]
                    bib, bankb, tyb = res["yb"]
                    DVE.wait(tsa, tya, tsb, tyb)
                    t1 = DVE.sig(nc.vector.tensor_tensor(out=gba[:], in0=gba[:], in1=banka[:], op=ALU.mult))
                    banks.free[bia] = t1
                    t2 = DVE.sig(nc.vector.tensor_tensor(out=gbb[:], in0=gbb[:], in1=bankb[:], op=ALU.mult))
                    banks.free[bib] = t2
                    DVE.wait(t2, mg_free)
                    t_mg = DVE.sig(nc.vector.tensor_tensor(out=mg[:, m, :], in0=gba[:], in1=gbb[:], op=ALU.add))
                    sg.free[gia] = t_mg
                    sg.free[gib] = t_mg
                    t_mg_last = t_mg
                    ta_ = LN.apply_pre(m)
                    if ta_ is not None:
                        r_free = ta_
                LN.apply_post(KC - 1)
                hb_free = res["gb"][2]
                in_free_tok_pe = res["yb"][2]
                if T + 1 < NT:
                    cast_tok = emit_cast4(T + 1)
                PE.wait(t_mg_last)
                DVE.wait(r_free)
                for m in range(KC):
                    msl = slice(m * 128, (m + 1) * 128)
                    bi, bank, bfree = banks.get()
                    PE.wait(bfree)
                    for k in range(KC):
                        ins = nc.tensor.matmul(bank[:], lhsT=wo[:, k, msl], rhs=mg[:, k, :], start=(k == 0), stop=(k == KC - 1))
                    t_mm = PE.sig(ins)
                    DVE.wait(t_mm)
                    t_r = DVE.sig(nc.vector.scalar_tensor_tensor(out=r[:, m, :], in0=hf[p][:, m, :], scalar=ALPHA, in1=bank[:], op0=ALU.mult, op1=ALU.add))
                    banks.free[bi] = t_r
                    stats_chunk(m, t_r, tsl)
                mg_free = t_mm
                in_free[p] = [t_r, in_free_tok_pe]
            run_deferred()
            barrier(LN.flush())

    store_tok_holder = [None]

    def phase_ffa(l, hsrc, after_loads=None):
        with ExitStack() as es:
            A = lambda name, shape, dt: es.enter_context(nc.sbuf_tensor(uniq(name), shape, dt))
            wf = A("wf", [128, KC, 2 * DFF], BF16)
            cwt = A("cwt", [128, 3, FC], F32)
            cbt = A("cbt", [128, FC], F32)
            halo = [A(f"halo{i}", [128, FC, 2], F32) for i in range(2)]
            hf = [A(f"hf{i}", [128, KC, TT], F32) for i in range(2)]
            hb = [A(f"hb{i}", [128, KC, TT], BF16) for i in range(2)]
            gb = Ring([A(f"gb{i}", [128, TT + 2], F32) for i in range(3)])
            t1r = Ring([A(f"t1{i}", [128, TT], F32) for i in range(4)])
            ast = Ring([A(f"as{i}", [128, TT], BF16) for i in range(4)])
            banks = Ring([es.enter_context(nc.psum_tensor(uniq(f"p5b{i}"), [128, 512], F32)) for i in range(7)])
            t_wc = load_w_cast(wf, w_fi[l], KC, 2 * DFF, "w5s", col_split=1408, order=[0, 2, 1, 3])
            if after_loads is not None:
                after_loads()
            so_b = so("p5bias")
            SP.dma(cwt[:], cw_f[l], so_b)
            t_b = SP.dma(cbt[:], cb_f[l], so_b)
            nc.vector.memset(halo[0][:], 0.0)
            t_h0 = DVE.sig(nc.vector.memset(halo[1][:], 0.0))
            DVE.wait(t_b)
            POOL.wait(t_b)
            ACT.wait(t_h0, t_b)
            hsrc_v = hsrc.rearrange("(c p) t -> p c t", p=128)
            actt_v = ACTT.rearrange("(f p) t -> p f t", p=128)
            ld_sem = [so("p5ld0"), so("p5ld1")]
            ld_tok = [None, None]
            hf_free = [None, None]
            hb_free = [None, None]
            cast_tok = [None, None]
            ld_tok[0] = SP.dma(hf[0][:], hsrc_v[:, :, 0:TT], ld_sem[0])
            store_toks = {}

            def emit_cast(T_):
                p_ = T_ % 2
                ACT.wait(ld_tok[p_], hb_free[p_])
                nc.scalar.activation(out=hb[p_][:, 0:4, :], in_=hf[p_][:, 0:4, :], func=AF.Identity)
                tk_ = ACT.sig(nc.scalar.activation(out=hb[p_][:, 4:8, :], in_=hf[p_][:, 4:8, :], func=AF.Identity))
                hf_free[p_] = tk_
                return tk_

            def stage2(item):
                ti, t1b, t_3 = item[0], item[1], item[2]
                ACT.wait(t_3)
                item.append(ACT.sig(nc.scalar.activation(out=t1b[:], in_=t1b[:], func=AF.Silu)))

            def stage3(item):
                ti, t1b, t_3, bi2, bank_u, t_u, f_, tsl_, t_s = item
                ai, abuf, afree = ast.get()
                DVE.wait(t_s, t_u, afree)
                t_a = DVE.sig(nc.vector.tensor_tensor(out=abuf[:], in0=t1b[:], in1=bank_u[:], op=ALU.mult))
                banks.free[bi2] = t_a
                t1r.free[ti] = t_a
                SP.wait(t_a)
                tk = SP.dma(actt_v[:, f_, tsl_], abuf[:], so(f"p5st{ai}"))
                ast.free[ai] = tk
                store_toks[ai] = tk

            pend = []
            for T in range(NT):
                p = T % 2
                tsl = slice(T * TT, (T + 1) * TT)
                hin, hout = halo[T % 2], halo[(T + 1) % 2]
                if T + 1 < NT:
                    SP.wait(hf_free[1 - p])
                    ld_tok[1 - p] = SP.dma(hf[1 - p][:], hsrc_v[:, :, (T + 1) * TT:(T + 2) * TT], ld_sem[1 - p])
                if T == 0:
                    cast_tok[0] = emit_cast(0)
                PE.wait(cast_tok[p])
                for f in range(FC):
                    bi, bank_g, bfree = banks.get()
                    PE.wait(bfree, t_wc[(f * 128) // 1408], t_wc[(DFF + f * 128) // 1408])
                    for k in range(KC):
                        ins = nc.tensor.matmul(bank_g[:], lhsT=wf[:, k, f * 128:(f + 1) * 128], rhs=hb[p][:, k, :], start=(k == 0), stop=(k == KC - 1))
                    t_g = PE.sig(ins)
                    bi2, bank_u, bfree2 = banks.get()
                    PE.wait(bfree2)
                    for k in range(KC):
                        ins = nc.tensor.matmul(bank_u[:], lhsT=wf[:, k, DFF + f * 128:DFF + (f + 1) * 128], rhs=hb[p][:, k, :], start=(k == 0), stop=(k == KC - 1))
                    t_u = PE.sig(ins)
                    if f == FC - 1:
                        hb_free[p] = t_u
                    gi, gbuf, gfree = gb.get()
                    ACT.wait(t_g, gfree)
                    nc.scalar.activation(out=gbuf[:, 0:2], in_=hin[:, f, :], func=AF.Identity)
                    nc.scalar.activation(out=gbuf[:, 2:TT + 2], in_=bank_g[:], func=AF.Identity)
                    t_ge = ACT.sig(nc.scalar.activation(out=hout[:, f, :], in_=bank_g[:, TT - 2:TT], func=AF.Identity))
                    banks.free[bi] = t_ge
                    ti, t1b, tfree = t1r.get()
                    POOL.wait(t_ge, tfree)
                    t_1 = POOL.sig(nc.gpsimd.tensor_scalar(out=t1b[:], in0=gbuf[:, 2:TT + 2], scalar1=cwt[:, 2, f:f + 1], scalar2=cbt[:, f:f + 1],
                                                           op0=ALU.mult, op1=ALU.add))
                    DVE.wait(t_1, t_ge)
                    t_2 = DVE.sig(nc.vector.scalar_tensor_tensor(out=t1b[:], in0=gbuf[:, 1:TT + 1], scalar=cwt[:, 1, f:f + 1], in1=t1b[:], op0=ALU.mult, op1=ALU.add))
                    DVE.wait(t_2)
                    t_3 = DVE.sig(nc.vector.scalar_tensor_tensor(out=t1b[:], in0=gbuf[:, 0:TT], scalar=cwt[:, 0, f:f + 1], in1=t1b[:], op0=ALU.mult, op1=ALU.add))
                    gb.free[gi] = t_3
                    item = [ti, t1b, t_3, bi2, bank_u, t_u, f, tsl]
                    if pend:
                        prev = pend.pop(0)
                        stage2(prev)
                        stage3(prev)
                    pend.append(item)
                    if f == FC // 2 and T + 1 < NT:
                        cast_tok[1 - p] = emit_cast(T + 1)
            while pend:
                prev = pend.pop(0)
                stage2(prev)
                stage3(prev)
            barrier(list(store_toks.values()))

    def phase_ffb(l, hsrc, hdst, pre):
        with ExitStack() as es:
            A = lambda name, shape, dt: es.enter_context(nc.sbuf_tensor(uniq(name), shape, dt))
            wfo, t_w = pre
            g_t = A("g_t", [128, KC], F32)
            b_t = A("b_t", [128, KC], F32)
            hf = [A(f"hf{i}", [128, KC, TT], F32) for i in range(2)]
            at = [A(f"at{i}", [128, FC, TT], BF16) for i in range(2)]
            r = A("r", [128, KC, TT], F32)
            sq = Ring([A(f"sq{i}", [128, TT], F32) for i in range(4)])
            outf = A("outf", [128, KC, TT], F32)
            lnb = [A(f"ln{i}", [128, TT], F32) for i in range(5)]
            tmp2 = [A(f"lt{i}", [128, TT], F32) for i in range(2)]
            banks = Ring([es.enter_context(nc.psum_tensor(uniq(f"p6b{i}"), [128, 512], F32)) for i in range(5)])
            ps_sum = es.enter_context(nc.psum_tensor(uniq("p6s"), [128, 512], F32))
            ps_sq = es.enter_context(nc.psum_tensor(uniq("p6q"), [128, 512], F32))
            so_b = so("p6bias")
            SP.dma(g_t[:], ln2g_f[l], so_b)
            t_b = SP.dma(b_t[:], ln2b_f[l], so_b)
            ACT.wait(t_b)
            PE.wait(t_w)
            hsrc_v = hsrc.rearrange("(c p) t -> p c t", p=128)
            hdst_v = hdst.rearrange("(c p) t -> p c t", p=128)
            actt_v = ACTT.rearrange("(f p) t -> p f t", p=128)
            ld_sem = [so("p6ld0"), so("p6ld1")]
            ld_tok = [None, None]
            in_free = [None, None]

            def load(T):
                p = T % 2
                SP.wait(in_free[p])
                tsl = slice(T * TT, (T + 1) * TT)
                SP.dma(hf[p][:], hsrc_v[:, :, tsl], ld_sem[p])
                ld_tok[p] = SP.dma(at[p][:], actt_v[:, :, tsl], ld_sem[p])

            load(0)
            LN = LNState((lnb[0], lnb[1], lnb[2], lnb[3], lnb[4], tmp2), r, outf, g_t, b_t, hdst_v, "p6st")
            stats_chunk, run_deferred = make_stats(r, sq, A("acc_r", [128, TT], F32), A("acc_q", [128, TT], F32), ps_sum, ps_sq, LN)
            for T in range(NT):
                p = T % 2
                tsl = slice(T * TT, (T + 1) * TT)
                if T + 1 < NT:
                    load(T + 1)
                PE.wait(ld_tok[p])
                DVE.wait(ld_tok[p])
                for m in range(KC):
                    msl = slice(m * 128, (m + 1) * 128)
                    bi, bank, bfree = banks.get()
                    PE.wait(bfree)
                    for f in range(FC):
                        ins = nc.tensor.matmul(bank[:], lhsT=wfo[:, f, msl], rhs=at[p][:, f, :], start=(f == 0), stop=(f == FC - 1))
                    t_mm = PE.sig(ins)
                    if m == 0:
                        run_deferred()
                    if m > 0:
                        LN.apply_post(m - 1)
                    ta_ = LN.apply_pre(m)
                    DVE.wait(t_mm, ta_)
                    t_r = DVE.sig(nc.vector.scalar_tensor_tensor(out=r[:, m, :], in0=hf[p][:, m, :], scalar=ALPHA, in1=bank[:], op0=ALU.mult, op1=ALU.add))
                    banks.free[bi] = t_r
                    stats_chunk(m, t_r, tsl)
                LN.apply_post(KC - 1)
                in_free[p] = [t_r, t_mm]
            run_deferred()
            barrier(LN.flush())

    hcur = xT
    for l in range(n_layers):
        last = (l == n_layers - 1)
        ph = 6 * l
        if ph < stop_phase:
            phase1(l, hcur)
        if ph + 1 < stop_phase:
            phase_swa(l)
        with ExitStack() as ws:
            wg = ws.enter_context(nc.sbuf_tensor(uniq("wg"), [128, KC, 2048], BF16))
            wpa = ws.enter_context(nc.sbuf_tensor(uniq("wpa"), [128, 4, D], BF16))
            wpb = ws.enter_context(nc.sbuf_tensor(uniq("wpb"), [128, 4, D], BF16))
            wo = ws.enter_context(nc.sbuf_tensor(uniq("wo"), [128, KC, D], BF16))
            tw = [None]

            def load_post():
                load_w_cast(wg, w_in[l, :, O_GA:O_GA + 2048], KC, 2048, "w4s", col_split=1024)
                load_w_cast(wpa, w_pa[l], 4, D, "w4s", col_split=1024)
                load_w_cast(wpb, w_pb[l], 4, D, "w4s", col_split=1024)
                tw[0] = load_w_cast(wo, w_o[l], KC, D, "w4s", col_split=1024)

            if ph + 2 < stop_phase:
                phase_fox(l, after_loads=load_post)
            if ph + 3 < stop_phase:
                if tw[0] is None:
                    load_post()
                phase_post(l, hcur, H1T, (wg, wpa, wpb, wo, tw[0]))
        with ExitStack() as ws:
            wfo = ws.enter_context(nc.sbuf_tensor(uniq("wfo"), [128, FC, D], BF16))
            tw = [None]

            def load_ffo():
                tw[0] = load_w_cast(wfo, w_fo[l], FC, D, "w6s", col_split=1024)

            if ph + 4 < stop_phase:
                phase_ffa(l, H1T, after_loads=load_ffo)
            if ph + 5 < stop_phase:
                if tw[0] is None:
                    load_ffo()
                phase_ffb(l, H1T, outT if last else H2T, (wfo, tw[0]))
        hcur = H2T
    top.close()
    return nc


def make_consts():
    ident = np.eye(128, dtype=np.float32)
    k = np.arange(128)[:, None]
    q = np.arange(128)[None, :]
    negm = np.where(k <= q, 0.0, -30000.0).astype(np.float32)
    slopes = (2.0 ** (-8.0 * np.arange(1, 9) / 8)).astype(np.float64)
    swam = np.zeros((128, 2, 2, 512), np.float32)
    for g in range(2):
        for hh in range(4):
            s = slopes[g * 4 + hh]
            dist_prev = (q + 128 - k).astype(np.float64)
            mp = np.where(dist_prev < 128, -8.0 * s * dist_prev, -30000.0)
            dist_cur = (q - k).astype(np.float64)
            mc = np.where(dist_cur >= 0, -8.0 * s * dist_cur, -30000.0)
            swam[:, g, 0, hh * 128:(hh + 1) * 128] = mp
            swam[:, g, 1, hh * 128:(hh + 1) * 128] = mc
    return ident, negm, swam


def host_layout(inputs):
    f = lambda a: np.ascontiguousarray(np.asarray(a, dtype=np.float32))
    b_in = f(inputs["b_in"])
    L = b_in.shape[0]
    fm = lambda v: np.ascontiguousarray(v.reshape(L, -1, 128).transpose(0, 2, 1))
    bq = np.concatenate([fm(b_in[:, O_QA:O_QA + 512]), fm(b_in[:, O_KA:O_KA + 128]), fm(b_in[:, O_QB:O_QB + 512]),
                         fm(b_in[:, O_KB:O_KB + 512])], axis=2)
    out = dict(
        b_q=f(bq), b_g=fm(b_in[:, O_GA:O_GA + 2048]), b_f=f(b_in[:, O_F:O_F + 8].reshape(L, 8, 1)),
        b_v=f(np.concatenate([b_in[:, O_VA:O_VA + 128], b_in[:, O_VB:O_VB + 512]], axis=1).reshape(L, 1, 640)),
        ln1g_f=fm(f(inputs["ln_mix_g"])), ln1b_f=fm(f(inputs["ln_mix_b"])),
        ln2g_f=fm(f(inputs["ln_ffn_g"])), ln2b_f=fm(f(inputs["ln_ffn_b"])),
        cw_f=f(f(inputs["conv_w"]).reshape(L, 3, FC, 128).transpose(0, 3, 1, 2)),
        cb_f=fm(f(inputs["conv_b"])),
    )
    return out


_WNAMES = ["w_in", "b_in", "attn_sinks", "w_proj_a", "w_proj_b", "w_out", "ln_mix_g", "ln_mix_b",
           "ln_ffn_g", "ln_ffn_b", "w_ffn_in", "conv_w", "conv_b", "w_ffn_out"]


def kernel(**inputs):
    x = np.asarray(inputs["x"], dtype=np.float32)
    ident, negm, swam = make_consts()
    nc = build_program()
    common = {n: np.ascontiguousarray(np.asarray(inputs[n], dtype=np.float32)) for n in _WNAMES}
    common.update(c_ident=ident, c_negm=negm, c_swam=swam)
    common.update(host_layout(inputs))
    in_maps = []
    for b in range(8):
        m = dict(common)
        m["xT"] = np.ascontiguousarray(x[b].T)
        in_maps.append(m)
    res = run_bass_kernel_spmd(nc, in_maps, core_ids=list(range(8)))
    out = np.stack([np.ascontiguousarray(res.results[b]["outT"].T) for b in range(8)], axis=0)
    return out.astype(np.float32)
```
